# Optimizing a Trainium2 kernel written in Bass

```python
import jax, jax.numpy as jnp
from jax import lax
import numpy as np

D_MODEL = 2048
BATCH = 2
SEQ = 4096
DEPTH = 2
DEC_BATCH = 8
DEC_SEQ = 8
PAST_LEN = 16384
PAGE_SIZE = 128

N_HEADS = 8
HEAD_DIM = 128
ATT_WIDTH = N_HEADS * HEAD_DIM
CONV_WIDTH = D_MODEL - ATT_WIDTH
CONV_K = 31
D_FF = 4 * D_MODEL
PLE_DIM = 256
Q_BLOCK = 128
EPS = 1e-6
IN_WIDTH = 3 * ATT_WIDTH + N_HEADS + 2 * CONV_WIDTH
FORGET_BIAS_MIN = 3.0
FORGET_BIAS_MAX = 10.0

kernel_name = 'fox_conformer_hybrid_step'


def rms_norm(x, g):
    xf = x.astype(jnp.float32)
    y = xf * lax.rsqrt(jnp.mean(xf * xf, axis=-1, keepdims=True) + EPS)
    return (y * g.astype(jnp.float32)).astype(x.dtype)


def layer_norm(x, g, b):
    xf = x.astype(jnp.float32)
    mu = jnp.mean(xf, axis=-1, keepdims=True)
    xc = xf - mu
    y = xc * lax.rsqrt(jnp.mean(xc * xc, axis=-1, keepdims=True) + EPS)
    return (y * g.astype(jnp.float32) + b.astype(jnp.float32)).astype(x.dtype)


def mixer_inputs(x, g_mix, w_in, b_f):
    h = rms_norm(x, g_mix) @ w_in
    B, L = x.shape[0], x.shape[1]
    q, k, v, f_logit, c_in = jnp.split(
        h, [ATT_WIDTH, 2 * ATT_WIDTH, 3 * ATT_WIDTH, 3 * ATT_WIDTH + N_HEADS], axis=-1)
    q = q.reshape(B, L, N_HEADS, HEAD_DIM)
    k = k.reshape(B, L, N_HEADS, HEAD_DIM)
    v = v.reshape(B, L, N_HEADS, HEAD_DIM)
    logf = jax.nn.log_sigmoid((f_logit + b_f).astype(jnp.float32))
    u = c_in[..., :CONV_WIDTH] * jax.nn.sigmoid(c_in[..., CONV_WIDTH:])
    return q, k, v, logf, u


def fox_attention_prompt(q, k, v, logf):
    B, L = q.shape[0], q.shape[1]
    scale = HEAD_DIM ** -0.5
    c = jnp.cumsum(logf, axis=1).transpose(0, 2, 1)
    kpos = jnp.arange(L)

    def block(i):
        start = i * Q_BLOCK
        qs = lax.dynamic_slice_in_dim(q, start, Q_BLOCK, axis=1)
        cq = lax.dynamic_slice_in_dim(c, start, Q_BLOCK, axis=2)
        s = jnp.einsum('bqhd,bkhd->bhqk', qs, k).astype(jnp.float32) * scale
        s = s + cq[..., :, None] - c[..., None, :]
        qpos = start + jnp.arange(Q_BLOCK)
        s = jnp.where(kpos[None, :] <= qpos[:, None], s, -jnp.inf)
        p = jax.nn.softmax(s, axis=-1).astype(v.dtype)
        return jnp.einsum('bhqk,bkhd->bqhd', p, v)

    out = lax.map(block, jnp.arange(L // Q_BLOCK))
    return out.transpose(1, 0, 2, 3, 4).reshape(B, L, N_HEADS, HEAD_DIM)


def fox_attention_sample(q, k_new, v_new, logf_new, k_past, v_past, logf_past):
    T = q.shape[1]
    P = k_past.shape[1]
    scale = HEAD_DIM ** -0.5
    lp = logf_past.astype(jnp.float32)
    r_past = (lax.cumsum(lp, axis=1, reverse=True) - lp).transpose(0, 2, 1)
    cn = jnp.cumsum(logf_new, axis=1).transpose(0, 2, 1)
    s_past = jnp.einsum('bqhd,bkhd->bhqk', q, k_past).astype(jnp.float32) * scale
    s_past = s_past + cn[..., :, None] + r_past[..., None, :]
    s_new = jnp.einsum('bqhd,bkhd->bhqk', q, k_new).astype(jnp.float32) * scale
    s_new = s_new + cn[..., :, None] - cn[..., None, :]
    causal = jnp.arange(T)[None, :] <= jnp.arange(T)[:, None]
    s_new = jnp.where(causal, s_new, -jnp.inf)
    p = jax.nn.softmax(jnp.concatenate([s_past, s_new], axis=-1), axis=-1).astype(v_new.dtype)
    return (jnp.einsum('bhqk,bkhd->bqhd', p[..., :P], v_past)
            + jnp.einsum('bhqk,bkhd->bqhd', p[..., P:], v_new))


def conv_heads(u_ext, conv_w, conv_b, ln_g, ln_b):
    y = lax.conv_general_dilated(
        u_ext, conv_w[:, None, :], window_strides=(1,), padding='VALID',
        dimension_numbers=('NWC', 'WIO', 'NWC'), feature_group_count=CONV_WIDTH)
    y = layer_norm(y + conv_b, ln_g, ln_b)
    return jax.nn.silu(y)


def finish_layer(x, attn, conv_y, pe, g_attn_out, w_out, g_mlp, w_up, w_down, g_ple, w_ple, w_ple_gate):
    B, L = x.shape[0], x.shape[1]
    attn = rms_norm(attn, g_attn_out.reshape(N_HEADS, HEAD_DIM)).reshape(B, L, ATT_WIDTH)
    x = x + jnp.concatenate([attn, conv_y], axis=-1) @ w_out
    hid = jnp.square(jax.nn.relu(rms_norm(x, g_mlp) @ w_up))
    x = x + hid @ w_down
    gate = jax.nn.sigmoid(rms_norm(x, g_ple) @ w_ple_gate)
    return x + (pe @ w_ple) * gate


def setup_inputs(seed: int = 0) -> dict:
    key = jax.random.key(seed)
    ks = jax.random.split(key, 32)
    n_pages = PAST_LEN // PAGE_SIZE
    n_used = DEC_BATCH * n_pages
    n_phys = (5 * n_used + 3) // 4
    f32 = jnp.float32
    nrm = lambda k, shape, s: (jax.random.normal(k, shape, f32) * s).astype(f32)
    perm = jax.random.permutation(ks[0], n_phys)[:n_used]
    page_table = perm.reshape(DEC_BATCH, n_pages).astype(jnp.int32)
    bias_base = jnp.linspace(FORGET_BIAS_MIN, FORGET_BIAS_MAX, N_HEADS, dtype=f32)
    cache_logf = jax.nn.log_sigmoid(
        bias_base + 0.5 * jax.random.normal(ks[5], (DEPTH, n_phys, PAGE_SIZE, N_HEADS), f32))
    return {
        'x_prompt': nrm(ks[1], (BATCH, SEQ, D_MODEL), 1.0),
        'x_sample': nrm(ks[2], (DEC_BATCH, DEC_SEQ, D_MODEL), 1.0),
        'cache_k': nrm(ks[3], (DEPTH, n_phys, PAGE_SIZE, N_HEADS, HEAD_DIM), 1.0),
        'cache_v': nrm(ks[4], (DEPTH, n_phys, PAGE_SIZE, N_HEADS, HEAD_DIM), 1.0),
        'cache_logf': cache_logf.astype(f32),
        'state_conv': nrm(ks[6], (DEPTH, DEC_BATCH, CONV_K - 1, CONV_WIDTH), 0.5),
        'page_table': page_table,
        'p_prompt': nrm(ks[7], (DEPTH, BATCH, SEQ, PLE_DIM), 1.0),
        'p_sample': nrm(ks[8], (DEPTH, DEC_BATCH, DEC_SEQ, PLE_DIM), 1.0),
        'w_in': nrm(ks[9], (DEPTH, D_MODEL, IN_WIDTH), D_MODEL ** -0.5),
        'b_f': bias_base[None, :] + nrm(ks[10], (DEPTH, N_HEADS), 0.2),
        'conv_w': nrm(ks[11], (DEPTH, CONV_K, CONV_WIDTH), CONV_K ** -0.5),
        'conv_b': nrm(ks[12], (DEPTH, CONV_WIDTH), 0.01),
        'conv_ln_g': 1.0 + nrm(ks[13], (DEPTH, CONV_WIDTH), 0.1),
        'conv_ln_b': nrm(ks[14], (DEPTH, CONV_WIDTH), 0.01),
        'g_attn_out': 1.0 + nrm(ks[15], (DEPTH, ATT_WIDTH), 0.1),
        'w_out': nrm(ks[16], (DEPTH, D_MODEL, D_MODEL), D_MODEL ** -0.5),
        'g_mix': 1.0 + nrm(ks[17], (DEPTH, D_MODEL), 0.1),
        'g_mlp': 1.0 + nrm(ks[18], (DEPTH, D_MODEL), 0.1),
        'w_up': nrm(ks[19], (DEPTH, D_MODEL, D_FF), D_MODEL ** -0.5),
        'w_down': nrm(ks[20], (DEPTH, D_FF, D_MODEL), D_FF ** -0.5),
        'g_ple': 1.0 + nrm(ks[21], (DEPTH, D_MODEL), 0.1),
        'w_ple': nrm(ks[22], (DEPTH, PLE_DIM, D_MODEL), PLE_DIM ** -0.5),
        'w_ple_gate': nrm(ks[23], (DEPTH, D_MODEL, D_MODEL), D_MODEL ** -0.5),
        'g_final': 1.0 + nrm(ks[24], (D_MODEL,), 0.1),
    }


def reference(x_prompt, x_sample, cache_k, cache_v, cache_logf, state_conv, page_table,
              p_prompt, p_sample, w_in, b_f, conv_w, conv_b, conv_ln_g, conv_ln_b,
              g_attn_out, w_out, g_mix, g_mlp, w_up, w_down, g_ple, w_ple, w_ple_gate, g_final):
    dec_b, n_pages = page_table.shape
    page = cache_k.shape[2]
    past = n_pages * page
    xp, xs = x_prompt, x_sample
    kp, vp, lfp, cvp = [], [], [], []
    ksm, vsm, lfs, cvs = [], [], [], []
    for l in range(DEPTH):
        post = (g_attn_out[l], w_out[l], g_mlp[l], w_up[l], w_down[l], g_ple[l], w_ple[l], w_ple_gate[l])
        q, k, v, logf, u = mixer_inputs(xp, g_mix[l], w_in[l], b_f[l])
        attn = fox_attention_prompt(q, k, v, logf)
        u_ext = jnp.concatenate([jnp.zeros((u.shape[0], CONV_K - 1, CONV_WIDTH), u.dtype), u], axis=1)
        cy = conv_heads(u_ext, conv_w[l], conv_b[l], conv_ln_g[l], conv_ln_b[l])
        xp = finish_layer(xp, attn, cy, p_prompt[l], *post)
        kp.append(k); vp.append(v); lfp.append(logf); cvp.append(u_ext[:, -(CONV_K - 1):])
        q, k, v, logf, u = mixer_inputs(xs, g_mix[l], w_in[l], b_f[l])
        k_past = cache_k[l][page_table].reshape(dec_b, past, N_HEADS, HEAD_DIM)
        v_past = cache_v[l][page_table].reshape(dec_b, past, N_HEADS, HEAD_DIM)
        lf_past = cache_logf[l][page_table].reshape(dec_b, past, N_HEADS)
        attn = fox_attention_sample(q, k, v, logf, k_past, v_past, lf_past)
        u_ext = jnp.concatenate([state_conv[l].astype(u.dtype), u], axis=1)
        cy = conv_heads(u_ext, conv_w[l], conv_b[l], conv_ln_g[l], conv_ln_b[l])
        xs = finish_layer(xs, attn, cy, p_sample[l], *post)
        ksm.append(k); vsm.append(v); lfs.append(logf); cvs.append(u_ext[:, -(CONV_K - 1):])
    y_prompt = rms_norm(xp, g_final)
    y_sample = rms_norm(xs, g_final)
    return (y_prompt, y_sample,
            jnp.stack(kp), jnp.stack(vp), jnp.stack(lfp), jnp.stack(cvp),
            jnp.stack(ksm), jnp.stack(vsm), jnp.stack(lfs), jnp.stack(cvs))
```

```python
import numpy as np
from contextlib import ExitStack
import concourse.bass as bass
import concourse.mybir as mybir
from concourse.bass_utils import run_bass_kernel_spmd

F32 = mybir.dt.float32
BF16 = mybir.dt.bfloat16
I32 = mybir.dt.int32
AF = mybir.ActivationFunctionType
ALU = mybir.AluOpType
AX = mybir.AxisListType

D = 2048; KT = 16; NP = 1024; NS = 8; NT = NP + NS
L = 2; H = 8; DH = 128; CW = 1024; CK = 31; HALO = CK - 1
DFF = 8192; PLE = 256; NPHYS = 1280; NPG = 128
INW = 3 * 1024 + 8 + 2 * CW
EPS = 1e-6
SCALE = DH ** -0.5
CH = [(0, 344), (344, 688), (688, 1032)]
NEG = -30000.0
RS = 2048 + HALO + 16
ENGS = ("pe", "act", "dve", "pool", "sp")
DMA_POOL = 12


class Buf:
    __slots__ = ("name", "last_w", "readers", "excl")

    def __init__(self, name):
        self.name = name
        self.last_w = None
        self.readers = {}
        self.excl = False


class Op:
    __slots__ = ("eng", "fn", "deps", "signal", "sigval", "kind", "dsem", "dval")


class Prog:
    def __init__(self, nc):
        self.nc = nc
        self.ops = {e: [] for e in ENGS}
        self.dma_count = {q: [0] * DMA_POOL for q in ("pool", "sp")}
        self.dma_rr = {"pool": 0, "sp": 0}
        self.dma_last = {q: [None] * DMA_POOL for q in ("pool", "sp")}
        self.cc_count = 0
        self.cc_last = None
        self.phase = Buf("phase")

    def op(self, eng, fn, reads=(), writes=(), kind="c", nophase=False):
        o = Op()
        o.eng = eng; o.fn = fn; o.signal = False; o.sigval = None
        o.kind = kind; o.dsem = None; o.dval = None
        deps = []
        writes = list(writes) + [b for b in reads if b.excl and b not in writes]
        reads = [b for b in reads if not b.excl]
        if not nophase:
            reads.append(self.phase)
        for b in reads:
            if b.last_w is not None:
                deps.append(b.last_w)
        for b in writes:
            if b.last_w is not None:
                deps.append(b.last_w)
            deps.extend(b.readers.values())
        if kind == "dma":
            k = self.dma_rr[eng]
            self.dma_rr[eng] = (k + 1) % DMA_POOL
            self.dma_count[eng][k] += 1
            o.dsem = (eng, k)
            o.dval = 16 * self.dma_count[eng][k]
            if self.dma_last[eng][k] is not None:
                deps.append(self.dma_last[eng][k])
            self.dma_last[eng][k] = o
        elif kind == "cc":
            self.cc_count += 1
            o.dsem = ("cc", 0)
            o.dval = self.cc_count
            if self.cc_last is not None:
                deps.append(self.cc_last)
            self.cc_last = o
        fl = []
        for d in deps:
            if d is o:
                continue
            if d.kind == "c" and d.eng == eng and eng == "pe":
                continue
            fl.append(d)
        o.deps = fl
        for d in fl:
            if d.kind == "c":
                d.signal = True
        for b in reads:
            key = eng if kind == "c" else ("x", id(o))
            b.readers[key] = o
        for b in writes:
            b.last_w = o
            b.readers = {}
        self.ops[eng].append(o)
        return o

    def barrier(self, fn):
        self.op("dve", fn, writes=[self.phase], nophase=True)

    def emit(self):
        nc = self.nc
        with ExitStack() as es:
            csem = {e: es.enter_context(nc.semaphore("c_" + e)) for e in ("pe", "act", "dve", "pool")}
            dsem = {}
            for q in ("pool", "sp"):
                for k in range(DMA_POOL):
                    if self.dma_count[q][k] > 0:
                        dsem[(q, k)] = es.enter_context(nc.semaphore("d_%s%d" % (q, k)))
            dsem[("cc", 0)] = es.enter_context(nc.semaphore("ccs"))
            for e in ENGS:
                c = 0
                for o in self.ops[e]:
                    if o.kind == "c" and o.signal:
                        c += 1
                        o.sigval = c
            block = es.enter_context(nc.Block())
            prog = self

            def run(ename, eobj):
                waited = {}
                for o in prog.ops[ename]:
                    need = {}
                    for d in o.deps:
                        if d.kind == "c":
                            key = ("c", d.eng); val = d.sigval
                        else:
                            key = ("d",) + d.dsem; val = d.dval
                        if need.get(key, 0) < val:
                            need[key] = val
                    for key, val in need.items():
                        if waited.get(key, 0) >= val:
                            continue
                        waited[key] = val
                        sem = dsem[key[1:]] if key[0] == "d" else csem[key[1]]
                        eobj.wait_ge(sem, val)
                    ins = o.fn(eobj)
                    if o.kind == "dma":
                        ins.then_inc(dsem[o.dsem], 16)
                    elif o.kind == "cc":
                        ins.then_inc(dsem[o.dsem])
                    elif o.signal:
                        ins.then_inc(csem[ename], 1)
                if ename in ("pool", "sp"):
                    for k in range(DMA_POOL):
                        if prog.dma_count[ename][k] > 0:
                            eobj.wait_ge(dsem[(ename, k)], 16 * prog.dma_count[ename][k])
                    if ename == "pool" and prog.cc_count:
                        eobj.wait_ge(dsem[("cc", 0)], prog.cc_count)

            @block.tensor
            def _(e):
                run("pe", e)

            @block.scalar
            def _(e):
                run("act", e)

            @block.vector
            def _(e):
                run("dve", e)

            @block.gpsimd
            def _(e):
                run("pool", e)

            @block.sync
            def _(e):
                run("sp", e)


C_ID = 0; C_TRI = 128; C_SUP = 256; C_UPS = 384; C_DM = 512; C_CAUS = 2560; C_OFF64 = 2568
C_OFF8 = 2632; C_ONE = 2640; NCST = 2768


def make_consts():
    c = np.zeros((128, NCST), np.float32)
    i = np.arange(128)
    c[:, C_ID:C_ID + 128] = np.eye(128)
    c[:, C_TRI:C_TRI + 128] = (i[:, None] <= i[None, :])
    c[:, C_SUP:C_SUP + 128] = (i[:, None] > i[None, :])
    c[:, C_UPS:C_UPS + 128] = (i[:, None] > i[None, :])
    f = np.arange(512)
    for k in range(4):
        c[:, C_DM + 512 * k:C_DM + 512 * (k + 1)] = (128 * k + i[:, None] <= f[None, :])
    c[:8, C_CAUS:C_CAUS + 8] = (np.arange(8)[:, None] <= np.arange(8)[None, :])
    c[:, C_OFF64:C_OFF64 + 64] = np.arange(64)[None, :]
    c[:, C_OFF8:C_OFF8 + 8] = np.arange(8)[None, :]
    c[:, C_ONE:C_ONE + 128] = 1.0
    return c


PL_GMIX = 0; PL_GMLP = 16; PL_GPLE = 32; PL_CW = 48; PL_CB = 296; PL_LG = 304; PL_LB = 312
PL_GA = 320; PL_BF = 328; PL_N = 336
P_GFIN = 2 * PL_N; P_PREV = P_GFIN + 16; P_OWN = P_PREV + 4; P_MB = P_OWN + 4; P_WQ = P_MB + 32
NPAR = P_WQ + 64


def pack_params(inp, j):
    p = np.zeros((128, NPAR), np.float32)
    fm = lambda v: np.ascontiguousarray(v.reshape(-1, 128).T)
    for l in range(L):
        o = l * PL_N
        p[:, o + PL_GMIX:o + PL_GMIX + 16] = fm(inp["g_mix"][l])
        p[:, o + PL_GMLP:o + PL_GMLP + 16] = fm(inp["g_mlp"][l])
        p[:, o + PL_GPLE:o + PL_GPLE + 16] = fm(inp["g_ple"][l])
        cw = inp["conv_w"][l]
        p[:, o + PL_CW:o + PL_CW + 248] = cw.reshape(CK, 8, 128).transpose(2, 1, 0).reshape(128, 248)
        p[:, o + PL_CB:o + PL_CB + 8] = fm(inp["conv_b"][l])
        p[:, o + PL_LG:o + PL_LG + 8] = fm(inp["conv_ln_g"][l])
        p[:, o + PL_LB:o + PL_LB + 8] = fm(inp["conv_ln_b"][l])
        p[:, o + PL_GA:o + PL_GA + 8] = fm(inp["g_attn_out"][l])
        p[:, o + PL_BF:o + PL_BF + 8] = inp["b_f"][l][None, :]
    p[:, P_GFIN:P_GFIN + 16] = fm(inp["g_final"])
    for r in range(4):
        p[:, P_PREV + r] = 1.0 if r == j - 1 else 0.0
        p[:, P_OWN + r] = 1.0 if r == j else 0.0
        p[:, P_MB + 8 * r:P_MB + 8 * r + 8] = 0.0 if r < j else NEG
    for qc in range(2):
        nblk = 8 * j + 4 * qc
        p[:, P_WQ + 32 * qc:P_WQ + 32 * qc + nblk] = 1.0
    return p


class _Stop(Exception):
    pass


KSTOP = None
import os as _os
DBG = set(_os.environ.get('KDBG', '').split(','))


def build_program():
    try:
        return _build_program()
    finally:
        pass


def _build_program():
    nc = bass.Bass("TRN2", target_bir_lowering=False, num_devices=8)
    dt_in = lambda n, s, d=F32: nc.dram_tensor(n, s, d, kind="ExternalInput").ap()
    dt_out = lambda n, s, d=F32: nc.dram_tensor(n, s, d, kind="ExternalOutput").ap()
    xT = dt_in("xT", [D, NT])
    pT = dt_in("pT", [L * PLE, NT])
    scv = dt_in("scv", [L * CW, HALO])
    par = dt_in("par", [128, NPAR])
    cst = dt_in("cst", [128, NCST])
    ptab = dt_in("ptab", [128, 1], I32)
    w_in = dt_in("w_in", [L * D, INW])
    if "TINYW" in DBG:
        w_out = dt_in("w_out", [128, 128]); w_up = dt_in("w_up", [128, 128]); w_down = dt_in("w_down", [128, 128])
        w_gate = dt_in("w_gate", [128, 128]); w_ple = dt_in("w_ple", [128, 128])
    else:
        w_out = dt_in("w_out", [L * D, D])
        w_up = dt_in("w_up", [L * D, DFF])
        w_down = dt_in("w_down", [L * DFF, D])
        w_gate = dt_in("w_gate", [L * D, D])
        w_ple = dt_in("w_ple", [L * PLE, D])
    ck = [[dt_in("ck%d_%d" % (l, hh), [NPHYS * 64, 1024]) for hh in range(2)] for l in range(L)]
    cv = [[dt_in("cv%d_%d" % (l, hh), [NPHYS * 64, 1024]) for hh in range(2)] for l in range(L)]
    clf = [dt_in("clf%d" % l, [NPHYS * 8, 128]) for l in range(L)]
    o_y = dt_out("o_y", [D, NT])
    o_k = dt_out("o_k", [L * 1024, NT])
    o_v = dt_out("o_v", [L * NT, 1024])
    o_lf = dt_out("o_lf", [L * NT, 8])
    o_cv = dt_out("o_cv", [L * CW, 2 * HALO])
    o_dbg = dt_out("o_dbg", [128, 16 * NS]) if "DBGOUT" in DBG else None
    o_dbg2 = dt_out("o_dbg2", [128, 16 * NS]) if "DBGOUT" in DBG else None
    sndK = [nc.dram_tensor("sndK%d" % i, [512, 1024], BF16, kind="Internal").ap() for i in range(2)]
    rcvK = [nc.dram_tensor("rcvK%d" % i, [4 * 512, 1024], BF16, kind="Internal").ap() for i in range(2)]
    sndV = [nc.dram_tensor("sndV%d" % i, [512, 1024], BF16, kind="Internal").ap() for i in range(2)]
    rcvV = [nc.dram_tensor("rcvV%d" % i, [4 * 512, 1024], BF16, kind="Internal").ap() for i in range(2)]
    sndM = nc.dram_tensor("sndM", [64, 1024], BF16, kind="Internal").ap()
    rcvM = nc.dram_tensor("rcvM", [4 * 64, 1024], BF16, kind="Internal").ap()
    LFR = 2048 + HALO
    sndf = sndM[32:48, :].bitcast(F32).rearrange("r (q h) -> (r q) h", h=8)
    rcvf = [rcvM[r * 64 + 32:r * 64 + 48, :].bitcast(F32).rearrange("r (q h) -> (r q) h", h=8) for r in range(4)]
    RG = [[0, 1, 2, 3], [4, 5, 6, 7]]

    P = Prog(nc)
    es = ExitStack()
    sb = lambda n, s, d: es.enter_context(nc.sbuf_tensor(n, s, d))
    X = sb("X", [128, KT, NT], F32)
    XN = sb("XN", [128, KT, NT], BF16)
    REG = sb("REG", [128, 16, NT + HALO], BF16)
    PAR = sb("PAR", [128, NPAR], F32)
    CF = sb("CF", [128, 512], F32)
    ONE32 = sb("ONE32", [128, 128], F32)
    IDB = sb("IDB", [128, 128], BF16)
    ONEB = sb("ONEB", [128, 128], BF16)
    DMB = sb("DMB", [128, 4, 512], BF16)
    CAUS = sb("CAUS", [128, 8], F32)
    OFF = sb("OFF", [128, 72], F32)
    WS = [sb("WS%d" % i, [128, 16, 256], BF16) for i in range(2)]
    WF = sb("WF", [128, 16, 8], BF16)
    WP = sb("WP", [128, 2, 256], BF16)
    PTE = sb("PTE", [128, 2, NT], BF16)
    KTS = sb("KTS", [128, H, NS], BF16)
    VTS = sb("VTS", [128, 1024], BF16)
    LF = sb("LF", [128, 9, 8], F32)
    US32 = sb("US32", [128, 8, HALO + NS], F32)
    USB = sb("USB", [128, 8, HALO + NS], BF16)
    CVO = sb("CVO", [128, 8, HALO], F32)
    SCF = sb("SCF", [128, 6400], F32)
    SCB = sb("SCB", [128, 7808], BF16)
    SCI = sb("SCI", [128, 144], I32)
    PSB = [es.enter_context(nc.psum_tensor("ps%d" % i, [128, 512], F32)) for i in range(7)]
    PST = es.enter_context(nc.psum_tensor("pst", [128, 1024], BF16))

    B = {}

    def bf(name):
        if name not in B:
            B[name] = Buf(name)
        return B[name]
    bX = [bf("X%d" % k) for k in range(KT)]
    bXN = [bf("XN%d" % k) for k in range(KT)]
    bREG = [bf("REG%d" % k) for k in range(16)]
    bPS = [bf("PS%d" % k) for k in range(7)]
    bPST = bf("PST")
    for b_ in bPS + [bPST]:
        b_.excl = True
    bWS = [bf("WS0"), bf("WS1")]
    wsi = [0]
    psi = [0]

    scr = {"f": 0, "b": 0, "n": 0}

    def sf(n):
        o = scr["f"]; scr["f"] += n
        assert scr["f"] <= 6400, scr
        return SCF[:, o:o + n], bf("sf@%d" % o)

    def sbb(n):
        o = scr["b"]; scr["b"] += n
        assert scr["b"] <= 7808, scr
        return SCB[:, o:o + n], bf("sb@%d" % o)

    stage = [0]

    def ckpt(name):
        if _os.environ.get("KCUT") == name:
            raise _Stop()

    def barrier():
        if o_dbg is not None and _os.environ.get("KDUMP") == str(stage[0] + 1):
            d1, bd1 = sf(16 * NS); d2, bd2 = sf(16 * NS)
            cp("dve", d1.rearrange("p (k t) -> p k t", t=NS), XN[:, :, NP:NT], bXN, [bd1])
            dma("sp", o_dbg, d1, [bd1], [])
            cp("dve", d2.rearrange("p (k t) -> p k t", t=NS), X[:, :, NP:NT], bX, [bd2])
            dma("sp", o_dbg2, d2, [bd2], [])
        P.barrier(lambda e: e.memset(SCI[:, 2:3], 0))
        scr["f"] = 0; scr["b"] = 0
        stage[0] += 1
        if KSTOP is not None and stage[0] >= KSTOP:
            raise _Stop()

    def dma(q, out, in_, reads, writes, **kw):
        P.op(q, lambda e: e.dma_start(out=out, in_=in_, **kw), reads=reads, writes=writes, kind="dma")

    def act(out, in_, func, reads, writes, **kw):
        P.op("act", lambda e: e.activation(out=out, in_=in_, func=func, **kw), reads=reads, writes=writes)

    def tt(eng, out, in0, in1, op, reads, writes):
        P.op(eng, lambda e: e.tensor_tensor(out=out, in0=in0, in1=in1, op=op), reads=reads, writes=writes)

    def ts(eng, out, in0, s1, s2, op0, op1, reads, writes):
        if op1 is None:
            P.op(eng, lambda e: e.tensor_scalar(out=out, in0=in0, scalar1=s1, scalar2=None, op0=op0),
                 reads=reads, writes=writes)
        else:
            P.op(eng, lambda e: e.tensor_scalar(out=out, in0=in0, scalar1=s1, scalar2=s2, op0=op0, op1=op1),
                 reads=reads, writes=writes)

    def stt(eng, out, in0, scalar, in1, op0, op1, reads, writes):
        P.op(eng, lambda e: e.scalar_tensor_tensor(out=out, in0=in0, scalar=scalar, in1=in1, op0=op0, op1=op1),
             reads=reads, writes=writes)

    def cp(eng, out, in_, reads, writes):
        P.op(eng, lambda e: e.tensor_copy(out=out, in_=in_), reads=reads, writes=writes)

    def mm(out, pairs, reads, writes, start=True, stop=True):
        def fn(e):
            ins = None
            n = len(pairs)
            for i, (l_, r_) in enumerate(pairs):
                ins = e.matmul(out, lhsT=l_, rhs=r_, start=(start and i == 0), stop=(stop and i == n - 1))
            return ins
        P.op("pe", fn, reads=reads, writes=writes)

    def nextps():
        k = psi[0]; psi[0] = (k + 1) % 4
        return k

    def body():
        bPAR = bf("PAR"); bC = bf("CONST")
        dma("sp", PAR[:], par, [], [bPAR])
        t_c, b_c = sf(NCST)
        dma("sp", t_c, cst, [], [b_c])
        cp("dve", CF[:], t_c[:, 0:512], [b_c], [bC])
        cp("dve", ONE32[:], t_c[:, C_ONE:C_ONE + 128], [b_c], [bC])
        cp("dve", IDB[:], t_c[:, C_ID:C_ID + 128], [b_c], [bC])
        cp("dve", ONEB[:], t_c[:, C_ONE:C_ONE + 128], [b_c], [bC])
        cp("dve", DMB[:].rearrange("p a f -> p (a f)"), t_c[:, C_DM:C_DM + 2048], [b_c], [bC])
        cp("dve", CAUS[:], t_c[:, C_CAUS:C_CAUS + 8], [b_c], [bC])
        cp("dve", OFF[:], t_c[:, C_OFF64:C_OFF64 + 72], [b_c], [bC])
        for kt in range(KT):
            dma("sp", X[:, kt, :], xT[kt * 128:(kt + 1) * 128, :], [], [bX[kt]])
        bIDX = bf("IDX")
        dma("sp", SCI[:, 0:1], ptab, [], [bIDX])
        t_pf, b_pf = sf(144)
        offs, b_offs = sf(136)
        cp("dve", t_pf[:, 0:1], SCI[:, 0:1], [bIDX], [b_pf])
        cp("dve", offs[:, 0:64], OFF[:, 0:64], [bC], [b_offs])
        ts("dve", offs[:, 64:128], OFF[:, 0:64], 64.0, None, ALU.add, None, [bC], [b_offs])
        cp("dve", offs[:, 128:136], OFF[:, 64:72], [bC], [b_offs])
        ts("dve", t_pf[:, 1:2], t_pf[:, 0:1], 64.0, None, ALU.mult, None, [b_pf], [b_pf])
        ts("dve", t_pf[:, 8:72], offs[:, 0:64], t_pf[:, 1:2], None, ALU.add, None, [b_offs, b_pf], [b_pf])
        ts("dve", t_pf[:, 72:136], offs[:, 0:64], t_pf[:, 1:2], None, ALU.add, None, [b_offs, b_pf], [b_pf])
        ts("dve", t_pf[:, 2:3], t_pf[:, 0:1], 8.0, None, ALU.mult, None, [b_pf], [b_pf])
        ts("dve", t_pf[:, 136:144], offs[:, 128:136], t_pf[:, 2:3], None, ALU.add, None, [b_offs, b_pf], [b_pf])
        cp("dve", SCI[:, 8:144], t_pf[:, 8:144], [b_pf], [bIDX])
        IDXK = lambda h, sq: SCI[:, 8 + h * 16 + sq:8 + h * 16 + sq + 1]
        IDXL = lambda h: SCI[:, 136 + h:137 + h]
        barrier()

        def rmsnorm(gcol, out_fn):
            sq = [sbb(NT), sbb(NT)]
            pss = [4, 5, 6]
            for kt in range(KT):
                t_, b_ = sq[kt % 2]
                act(t_, X[:, kt, :], AF.Square, [bX[kt]], [b_])
                for n, (a, z) in enumerate(CH):
                    mm(PSB[pss[n]][:, 0:z - a], [(ONEB[:], t_[:, a:z])], [b_, bC], [bPS[pss[n]]],
                       start=(kt == 0), stop=(kt == KT - 1))
            rstd, b_r = sf(NT)
            for n, (a, z) in enumerate(CH):
                act(rstd[:, a:z], PSB[pss[n]][:, 0:z - a], AF.Ln, [bPS[pss[n]]], [b_r], scale=1.0 / D, bias=EPS)
            act(rstd, rstd, AF.Exp, [b_r], [b_r], scale=-0.5)
            for kt in range(KT):
                out_fn(kt, rstd, b_r, PAR[:, gcol + kt:gcol + kt + 1])

        def xn_out(kt, rstd, b_r, g):
            stt("dve", XN[:, kt, :], X[:, kt, :], g, rstd, ALU.mult, ALU.mult, [bX[kt], b_r, bPAR], [bXN[kt]])

        def load_w(slot, wap, row0, nkt, c0, ncols, dst_c0=0):
            src = wap[row0:row0 + nkt * 128, c0:c0 + ncols].rearrange("(k p) c -> p k c", p=128)
            dma("pool", WS[slot][:, 0:nkt, dst_c0:dst_c0 + ncols], src, [], [bWS[slot]])

        def proj_fm(wap, row0, col_tiles, nkt, rhs, rhs_bufs, evac):
            for t0 in range(0, len(col_tiles), 2):
                grp = col_tiles[t0:t0 + 2]
                slot = wsi[0]; wsi[0] ^= 1
                for gi, c0 in enumerate(grp):
                    load_w(slot, wap, row0, nkt, c0, 128, dst_c0=128 * gi)
                for gi, c0 in enumerate(grp):
                    for n, (a, z) in enumerate(CH):
                        k = nextps()
                        mm(PSB[k][:, 0:z - a],
                           [(WS[slot][:, kt, 128 * gi:128 * gi + 128], rhs(kt, a, z)) for kt in range(nkt)],
                           [bWS[slot]] + rhs_bufs, [bPS[k]])
                        evac(t0 + gi, n, a, z, PSB[k][:, 0:z - a], bPS[k])

        for l in range(L):
            po = l * PL_N
            bPTE = bf("PTE")
            pTl = pT[l * PLE:(l + 1) * PLE, :].rearrange("(k p) n -> p k n", p=128)
            for c0, c1 in ((0, 512), (512, NP), (NP, NT)):
                dma("pool", PTE[:, :, c0:c1], pTl[:, :, c0:c1], [], [bPTE])
            bUS = bf("US")
            dma("sp", US32[:, :, 0:HALO], scv[l * CW:(l + 1) * CW, :].rearrange("(c p) t -> p c t", p=128), [], [bUS])
            rmsnorm(po + PL_GMIX, xn_out)
            ckpt("A1")
            rhs_xn = lambda kt, a, z: XN[:, kt, a:z]
            QT = lambda h: REG[:, h, 0:NT]
            U = lambda c: REG[:, 8 + c, :]
            bCVO = bf("CVO")
            glu = {}

            def ev_conv(ti, n, a, z, ps, bps):
                c, isg = ti // 2, ti % 2
                if not isg:
                    v32, bv = sf(344)
                    cp("dve", v32[:, 0:z - a], ps, [bps], [bv])
                    glu[(c, n)] = (v32, bv)
                else:
                    v32, bv = glu[(c, n)]
                    sg, bs = sf(344)
                    act(sg[:, 0:z - a], ps, AF.Sigmoid, [bps], [bs])
                    tt("dve", v32[:, 0:z - a], v32[:, 0:z - a], sg[:, 0:z - a], ALU.mult, [bv, bs], [bv])
                    cp("pool", U(c)[:, HALO + a:HALO + z], v32[:, 0:z - a], [bv], [bREG[8 + c]])
                    if n == 2:
                        cp("pool", CVO[:, c, :], v32[:, 994 - a:1024 - a], [bv], [bCVO])
                        cp("pool", US32[:, c, HALO:HALO + NS], v32[:, 1024 - a:1032 - a], [bv], [bUS])
                    if (c * 3 + n) % 4 == 3:
                        scr["f"] -= 0
            base_f = scr["f"]
            ctiles = []
            for c in range(8):
                ctiles += [3080 + 128 * c, 3080 + 1024 + 128 * c]
            for c in range(8):
                scr["f"] = base_f + (c % 2) * 6 * 344
                proj_fm(w_in, l * D, ctiles[2 * c:2 * c + 2], KT, rhs_xn, bXN,
                        lambda ti, n, a, z, ps, bps, c=c: ev_conv(2 * c + ti, n, a, z, ps, bps))
            scr["f"] = base_f + 12 * 344
            ckpt("A2")
            cp("pool", USB[:], US32[:], [bUS], [bUS])
            dma("sp", o_cv[l * CW:(l + 1) * CW, 0:HALO].rearrange("(c p) t -> p c t", p=128), CVO[:], [bCVO], [])
            dma("sp", o_cv[l * CW:(l + 1) * CW, HALO:2 * HALO].rearrange("(c p) t -> p c t", p=128),
                US32[:, :, NS:NS + HALO], [bUS], [])
            bSND = bf("SND")
            for c in range(8):
                dst = sndM[0:HALO, :].rearrange("r c -> (r c)").rearrange("(c p t) -> p c t", p=128, t=HALO)
                dma("sp", dst[:, c, :], U(c)[:, HALO + 994:HALO + 1024], [bREG[8 + c]], [bSND])

            ckpt("A3")
            bKTS = bf("KTS")
            kb16 = [sbb(NT), sbb(NT)]
            k32 = [sf(344), sf(344)]
            cnt = [0]

            def ev_qk(ti, n, a, z, ps, bps):
                if ti < 8:
                    act(QT(ti)[:, a:z], ps, AF.Identity, [bps], [bREG[ti]])
                else:
                    h = ti - 8
                    t16, b16 = kb16[h % 2]
                    act(t16[:, a:z], ps, AF.Identity, [bps], [b16])
                    t32, b32 = k32[cnt[0] % 2]; cnt[0] += 1
                    cp("dve", t32[:, 0:z - a], ps, [bps], [b32])
                    if "NOKOUT" not in DBG:
                        dma("sp", o_k[l * 1024 + h * 128:l * 1024 + (h + 1) * 128, a:z], t32[:, 0:z - a], [b32], [])
                    if n == 2:
                        if "NOSNDK" not in DBG:
                            dma("sp", sndK[h // 4][(h % 4) * 128:(h % 4 + 1) * 128, :], t16[:, 0:NP], [b16], [bSND])
                        if "NOKTS" not in DBG:
                            cp("pool", KTS[:, h, :], t16[:, NP:NT], [b16], [bKTS])
            proj_fm(w_in, l * D, [128 * i for i in range(8 if "QONLY" in DBG else 16)], KT, rhs_xn, bXN, ev_qk)

            ckpt("A4")
            bWF = bf("WF"); bLF = bf("LF"); bVTS = bf("VTS")
            dma("pool", WF[:], w_in[l * D:(l + 1) * D, 3072:3080].rearrange("(k p) c -> p k c", p=128), [], [bWF])
            v32 = [sf(256), sf(256)]
            v16 = [sbb(256), sbb(256)]
            cnt2 = [0]
            for vs in range(4):
                slot = wsi[0]; wsi[0] ^= 1
                load_w(slot, w_in, l * D, KT, 2048 + 256 * vs, 128)
                load_w(slot, w_in, l * D, KT, 2048 + 256 * vs + 128, 128, dst_c0=128)
                for tb in range(9):
                    m = 128 if tb < 8 else NS
                    k = nextps()
                    mm(PSB[k][0:m, 0:256], [(XN[:, kt, tb * 128:tb * 128 + m], WS[slot][:, kt, :]) for kt in range(KT)],
                       [bWS[slot]] + bXN, [bPS[k]])
                    t32, b32 = v32[cnt2[0] % 2]; t16, b16 = v16[cnt2[0] % 2]; cnt2[0] += 1
                    cp("dve", t32[0:m, :], PSB[k][0:m, 0:256], [bPS[k]], [b32])
                    dma("sp", o_v[l * NT + tb * 128:l * NT + tb * 128 + m, 256 * vs:256 * (vs + 1)], t32[0:m, :], [b32], [])
                    if tb < 8:
                        act(t16[0:m, :], PSB[k][0:m, 0:256], AF.Identity, [bPS[k]], [b16])
                        dma("sp", sndV[tb // 4][(tb % 4) * 128:(tb % 4 + 1) * 128, 256 * vs:256 * (vs + 1)], t16[:, :], [b16], [bSND])
                    else:
                        act(VTS[0:m, 256 * vs:256 * (vs + 1)], PSB[k][0:m, 0:256], AF.Identity, [bPS[k]], [bVTS])
            for tb in range(9):
                m = 128 if tb < 8 else NS
                k = nextps()
                mm(PSB[k][0:m, 0:8], [(XN[:, kt, tb * 128:tb * 128 + m], WF[:, kt, :]) for kt in range(KT)],
                   [bWF] + bXN, [bPS[k]])
                tt("dve", LF[0:m, tb, :], PSB[k][0:m, 0:8], PAR[0:m, po + PL_BF:po + PL_BF + 8], ALU.add, [bPS[k], bPAR], [bLF])
            ckpt("A5")
            act(LF[:, 0:8, :], LF[:, 0:8, :], AF.Exp, [bLF], [bLF], scale=-1.0)
            act(LF[0:NS, 8, :], LF[0:NS, 8, :], AF.Exp, [bLF], [bLF], scale=-1.0)
            act(LF[:, 0:8, :], LF[:, 0:8, :], AF.Ln, [bLF], [bLF], bias=1.0)
            act(LF[0:NS, 8, :], LF[0:NS, 8, :], AF.Ln, [bLF], [bLF], bias=1.0)
            ts("dve", LF[:, 0:8, :], LF[:, 0:8, :], -1.0, None, ALU.mult, None, [bLF], [bLF])
            ts("dve", LF[0:NS, 8, :], LF[0:NS, 8, :], -1.0, None, ALU.mult, None, [bLF], [bLF])
            bSNDF = bf("SNDF")
            dma("sp", o_lf[l * NT:l * NT + NP, :].rearrange("(k p) h -> p k h", p=128), LF[:, 0:8, :], [bLF], [])
            dma("sp", o_lf[l * NT + NP:(l + 1) * NT, :], LF[0:NS, 8, :], [bLF], [])
            dma("sp", sndf.rearrange("(k p) h -> p k h", p=128), LF[:, 0:8, :], [bLF], [bSND])
            ckpt("A6")
            bRCV = bf("RCV"); bRCVF = bf("RCVF")
            if "NOCC" not in DBG:
                for s_, r_ in [(sndM, rcvM), (sndK[0], rcvK[0]), (sndK[1], rcvK[1]), (sndV[0], rcvV[0]), (sndV[1], rcvV[1])]:
                    P.op("pool", lambda e, s_=s_, r_=r_: e.collective_compute("AllGather", ALU.bypass, replica_groups=RG,
                         ins=[s_], outs=[r_]), reads=[bSND], writes=[bRCV], kind="cc")
            barrier()

            kg = [sbb(1024), sbb(1024)]
            vg = [sbb(1024), sbb(1024)]
            ktt, bktt = sbb(1024)
            pts, bpts = sbb(64)
            lg, blg = sf(128); lgt, blgt = sf(128); tb_, btb = sf(128); er, ber = sf(128)
            pex, bpex = sf(128)
            ncn, bncn = sf(8); ptn, bptn = sbb(8)
            tsum, btsum = sf(1)
            OTS, DENS, SP, RP = 0, 1, 2, 3
            mm(PSB[RP][0:NS, 0:8], [(CF[0:NS, C_TRI:C_TRI + NS], LF[0:NS, 8, :])], [bC, bLF], [bPS[RP]])
            ts("dve", ncn[0:NS, :], PSB[RP][0:NS, 0:8], -1.0, None, ALU.mult, None, [bPS[RP]], [bncn])
            gi = [0]
            for h in range(H):
                P.op("pool", lambda e, h=h, l=l, lg=lg: e.indirect_dma_start(out=lg, out_offset=None, in_=clf[l],
                     in_offset=bass.IndirectOffsetOnAxis(ap=IDXL(h), axis=0)), reads=[bIDX], writes=[blg], kind="dma")
                P.op("pe", lambda e, lg=lg: e.transpose(PSB[RP][:, 0:128], lg, CF[:, C_ID:C_ID + 128]), reads=[blg, bC], writes=[bPS[RP]])
                cp("dve", lgt, PSB[RP][:, 0:128], [bPS[RP]], [blgt])
                P.op("dve", lambda e, tsum=tsum, lg=lg: e.tensor_reduce(out=tsum, in_=lg, axis=AX.X, op=ALU.add), reads=[blg], writes=[btsum])
                ts("dve", tb_, ONE32[:], tsum[:, 0:1], None, ALU.mult, None, [bC, btsum], [btb])
                mm(PSB[RP][:, 0:128], [(lgt, CF[:, C_UPS:C_UPS + 128]), (CF[:, C_SUP:C_SUP + 128], tb_)],
                   [blgt, btb, bC], [bPS[RP]])
                act(er, PSB[RP][:, 0:128], AF.Exp, [bPS[RP]], [ber])
                for sq in range(16):
                    tk, btk = kg[gi[0] % 2]; tv, btv = vg[gi[0] % 2]; gi[0] += 1
                    P.op("pool", lambda e, tk=tk, h=h, sq=sq, l=l: e.indirect_dma_start(out=tk, out_offset=None, in_=ck[l][h // 4],
                         in_offset=bass.IndirectOffsetOnAxis(ap=IDXK(h, sq), axis=0)), reads=[bIDX], writes=[btk], kind="dma")
                    P.op("pool", lambda e, tv=tv, h=h, sq=sq, l=l: e.indirect_dma_start(out=tv, out_offset=None, in_=cv[l][h // 4],
                         in_offset=bass.IndirectOffsetOnAxis(ap=IDXK(h, sq), axis=0)), reads=[bIDX], writes=[btv], kind="dma")

                    def tfn(e, tk=tk):
                        ins = None
                        for s in range(8):
                            ins = e.transpose(PST[:, s * 128:(s + 1) * 128], tk[:, s * 128:(s + 1) * 128], IDB[:])
                        return ins
                    P.op("pe", tfn, reads=[btk, bC], writes=[bPST])
                    act(ktt, PST[:, :], AF.Identity, [bPST], [bktt])

                    def sfn(e, h=h, ktt=ktt, QT=QT):
                        ins = None
                        for s in range(8):
                            ins = e.matmul(PSB[SP][:, s * 8:(s + 1) * 8], lhsT=ktt[:, s * 128:(s + 1) * 128],
                                           rhs=QT(h)[:, NP:NT], start=True, stop=True)
                        return ins
                    P.op("pe", sfn, reads=[bktt, bREG[h]], writes=[bPS[SP]])
                    act(pex[:, 0:64], PSB[SP][:, 0:64], AF.Exp, [bPS[SP]], [bpex], scale=SCALE)
                    tt("dve", pts.rearrange("p (s t) -> p s t", t=NS), pex[:, 0:64].rearrange("p (s t) -> p s t", t=NS),
                       er[:, sq * 8:(sq + 1) * 8].unsqueeze(2).to_broadcast([128, 8, NS]), ALU.mult, [bpex, ber], [bpts])

                    def pvfn(e, h=h, sq=sq, tv=tv, pts=pts):
                        ins = None
                        for s in range(8):
                            st = (sq == 0 and s == 0)
                            e.matmul(PSB[OTS][:, h * 8:(h + 1) * 8], lhsT=tv[:, s * 128:(s + 1) * 128],
                                     rhs=pts[:, s * 8:(s + 1) * 8], start=st, stop=False)
                            ins = e.matmul(PSB[DENS][:, h * 8:(h + 1) * 8], lhsT=ONEB[:],
                                           rhs=pts[:, s * 8:(s + 1) * 8], start=st, stop=False)
                        return ins
                    P.op("pe", pvfn, reads=[btv, bpts, bC], writes=[bPS[OTS], bPS[DENS]])
                mm(PSB[SP][0:NS, 0:NS], [(KTS[:, h, :], QT(h)[:, NP:NT])], [bKTS, bREG[h]], [bPS[SP]])
                act(pex[0:NS, 0:NS], PSB[SP][0:NS, 0:NS], AF.Exp, [bPS[SP], bncn], [bpex], scale=SCALE, bias=ncn[0:NS, h:h + 1])
                tt("dve", ptn[0:NS, :], pex[0:NS, 0:NS], CAUS[0:NS, :], ALU.mult, [bpex, bC], [bptn])

                def nfn(e, h=h, ptn=ptn):
                    e.matmul(PSB[OTS][:, h * 8:(h + 1) * 8], lhsT=VTS[0:NS, h * 128:(h + 1) * 128], rhs=ptn[0:NS, :],
                             start=False, stop=True)
                    return e.matmul(PSB[DENS][:, h * 8:(h + 1) * 8], lhsT=ONEB[0:NS, :], rhs=ptn[0:NS, :],
                                    start=False, stop=True)
                P.op("pe", nfn, reads=[bVTS, bptn, bC], writes=[bPS[OTS], bPS[DENS]])
            rd, brd = sf(64); o32, bo32 = sf(64); osq, bosq = sbb(64); rs_, brs = sf(64)
            P.op("dve", lambda e, rd=rd: e.reciprocal(out=rd, in_=PSB[DENS][:, 0:64]), reads=[bPS[DENS]], writes=[brd])
            tt("dve", o32, PSB[OTS][:, 0:64], rd, ALU.mult, [bPS[OTS], brd], [bo32])
            act(osq, o32, AF.Square, [bo32], [bosq])
            mm(PSB[RP][:, 0:64], [(ONEB[:], osq)], [bosq, bC], [bPS[RP]])
            act(rs_, PSB[RP][:, 0:64], AF.Ln, [bPS[RP]], [brs], scale=1.0 / DH, bias=EPS)
            act(rs_, rs_, AF.Exp, [brs], [brs], scale=-0.5)
            tt("dve", o32, o32, rs_, ALU.mult, [bo32, brs], [bo32])
            tt("dve", XN[:, 0:8, NP:NT], o32.rearrange("p (h t) -> p h t", t=NS),
               PAR[:, po + PL_GA:po + PL_GA + 8].unsqueeze(2).to_broadcast([128, 8, NS]), ALU.mult,
               [bo32, bPAR], bXN[0:8])
            barrier()

            hal, bhal = sbb(4 * 8 * HALO)
            halv = hal.rearrange("p (r c t) -> p r c t", r=4, c=8)
            for r in range(4):
                src = rcvM[r * 64:r * 64 + HALO, :].rearrange("r c -> (r c)").rearrange("(c p t) -> p c t", p=128, t=HALO)
                dma("sp", halv[:, r], src, [bRCV], [bhal])
            for c in range(8):
                ts("dve", U(c)[:, 0:HALO], halv[:, 0, c, :], PAR[:, P_PREV:P_PREV + 1], None, ALU.mult, None, [bhal, bPAR], [bREG[8 + c]])
                for r in range(1, 4):
                    stt("dve", U(c)[:, 0:HALO], halv[:, r, c, :], PAR[:, P_PREV + r:P_PREV + r + 1], U(c)[:, 0:HALO],
                        ALU.mult, ALU.add, [bhal, bPAR], [bREG[8 + c]])
            dg = [sbb(128) for _ in range(4)]
            di = [0]
            for c in range(8):
                for j_ in range(CK):
                    td, bd = dg[di[0] % 4]; di[0] += 1
                    ts("pool", td, IDB[:], PAR[:, po + PL_CW + c * CK + j_:po + PL_CW + c * CK + j_ + 1], None, ALU.mult, None,
                       [bC, bPAR], [bd])

                    def cfn(e, td=td, c=c, j_=j_, U=U):
                        e.matmul(PSB[0][:, :], lhsT=td, rhs=U(c)[:, j_:j_ + 512], start=(j_ == 0), stop=(j_ == CK - 1))
                        e.matmul(PSB[1][:, :], lhsT=td, rhs=U(c)[:, 512 + j_:512 + j_ + 512], start=(j_ == 0), stop=(j_ == CK - 1))
                        return e.matmul(PSB[2][:, 0:NS], lhsT=td, rhs=USB[:, c, j_:j_ + NS], start=(j_ == 0), stop=(j_ == CK - 1))
                    P.op("pe", cfn, reads=[bd, bREG[8 + c], bUS], writes=[bPS[0], bPS[1], bPS[2]])
                cb = PAR[:, po + PL_CB + c:po + PL_CB + c + 1]
                act(XN[:, 8 + c, 0:512], PSB[0][:, :], AF.Identity, [bPS[0], bPAR], [bXN[8 + c]], bias=cb)
                act(XN[:, 8 + c, 512:1024], PSB[1][:, :], AF.Identity, [bPS[1], bPAR], [bXN[8 + c]], bias=cb)
                act(XN[:, 8 + c, NP:NT], PSB[2][:, 0:NS], AF.Identity, [bPS[2], bPAR], [bXN[8 + c]], bias=cb)
            sq2 = [sbb(NT), sbb(NT)]
            for c in range(8):
                t_, b_ = sq2[c % 2]
                act(t_, XN[:, 8 + c, :], AF.Square, [bXN[8 + c]], [b_])
                for n, (a, z) in enumerate(CH):
                    mm(PSB[n][:, 0:z - a], [(ONEB[:], XN[:, 8 + c, a:z])], [bXN[8 + c], bC], [bPS[n]], start=(c == 0), stop=(c == 7))
                    mm(PSB[3 + n][:, 0:z - a], [(ONEB[:], t_[:, a:z])], [b_, bC], [bPS[3 + n]], start=(c == 0), stop=(c == 7))
            mean, bmean = sf(NT); rstd2, brstd2 = sf(NT); tmp, btmp = sf(NT); sg2, bsg2 = sf(NT)
            for n, (a, z) in enumerate(CH):
                ts("dve", mean[:, a:z], PSB[n][:, 0:z - a], 1.0 / CW, None, ALU.mult, None, [bPS[n]], [bmean])
                tt("dve", tmp[:, a:z], mean[:, a:z], mean[:, a:z], ALU.mult, [bmean], [btmp])
                stt("dve", rstd2[:, a:z], PSB[3 + n][:, 0:z - a], 1.0 / CW, tmp[:, a:z], ALU.mult, ALU.subtract,
                    [bPS[3 + n], btmp], [brstd2])
            act(rstd2, rstd2, AF.Ln, [brstd2], [brstd2], bias=EPS)
            act(rstd2, rstd2, AF.Exp, [brstd2], [brstd2], scale=-0.5)
            for c in range(8):
                tt("dve", tmp, XN[:, 8 + c, :], mean, ALU.subtract, [bXN[8 + c], bmean], [btmp])
                tt("dve", tmp, tmp, rstd2, ALU.mult, [btmp, brstd2], [btmp])
                ts("dve", tmp, tmp, PAR[:, po + PL_LG + c:po + PL_LG + c + 1], PAR[:, po + PL_LB + c:po + PL_LB + c + 1],
                   ALU.mult, ALU.add, [btmp, bPAR], [btmp])
                act(sg2, tmp, AF.Sigmoid, [btmp], [bsg2])
                tt("dve", XN[:, 8 + c, :], tmp, sg2, ALU.mult, [btmp, bsg2], [bXN[8 + c]])
            barrier()

            if o_dbg is not None and l == 0 and "DBGX" not in DBG and "DBGX1" not in DBG:
                dg32, bdg32 = sf(16 * NS)
                cp("dve", dg32.rearrange("p (k t) -> p k t", t=NS), XN[:, :, NP:NT], bXN, [bdg32])
                dma("sp", o_dbg, dg32, [bdg32], [])
            lfa, blfa = sf(256); tot, btot = sf(256); cg, bcg = sf(256); car, bcar = sf(256)
            cref, bcref = sf(16); bias, bbias = sf(512); cgo, bcgo = sf(64); biaso, bbiaso = sf(128); wt, bwt = sf(256)
            lfv = lfa.rearrange("p (k h) -> p k h", h=8)
            for r in range(4):
                dma("sp", lfv[:, 8 * r:8 * r + 8, :], rcvf[r].rearrange("(k p) h -> p k h", p=128), [bRCV], [blfa])
            for blk in range(32):
                mm(PSB[4][:, blk * 8:(blk + 1) * 8], [(ONE32[:], lfv[:, blk, :])], [blfa, bC], [bPS[4]])
                mm(PSB[5][:, blk * 8:(blk + 1) * 8], [(CF[:, C_TRI:C_TRI + 128], lfv[:, blk, :])], [blfa, bC], [bPS[5]])
            cp("dve", tot, PSB[4][:, 0:256], [bPS[4]], [btot])
            P.op("dve", lambda e, car=car: e.memset(car[:, 0:8], 0.0), writes=[bcar])
            for blk in range(1, 32):
                tt("dve", car[:, blk * 8:(blk + 1) * 8], car[:, (blk - 1) * 8:blk * 8], tot[:, (blk - 1) * 8:blk * 8], ALU.add,
                   [bcar, btot], [bcar])
            tt("dve", cg, PSB[5][:, 0:256], car, ALU.add, [bPS[5], bcar], [bcg])
            for qc in range(2):
                tt("dve", wt.rearrange("p (h k) -> p h k", h=8), tot.rearrange("p (k h) -> p h k", h=8),
                   PAR[:, P_WQ + 32 * qc:P_WQ + 32 * (qc + 1)].unsqueeze(1).to_broadcast([128, 8, 32]), ALU.mult,
                   [btot, bPAR], [bwt])
                P.op("dve", lambda e, qc=qc, cref=cref, wt=wt: e.tensor_reduce(out=cref[:, qc * 8:(qc + 1) * 8], in_=wt.rearrange("p (h k) -> p h k", h=8),
                     axis=AX.X, op=ALU.add), reads=[bwt], writes=[bcref])
            biasv = bias.rearrange("p (q k h) -> p q k h", q=2, h=8)
            cgv = cg.rearrange("p (k h) -> p k h", h=8)
            for qc in range(2):
                tt("dve", biasv[:, qc], cref[:, qc * 8:(qc + 1) * 8].unsqueeze(1).to_broadcast([128, 32, 8]), cgv, ALU.subtract,
                   [bcref, bcg], [bbias])
                tt("dve", biasv[:, qc], biasv[:, qc], PAR[:, P_MB:P_MB + 32].unsqueeze(2).to_broadcast([128, 32, 8]), ALU.add,
                   [bbias, bPAR], [bbias])
            cgov = cgo.rearrange("p (k h) -> p k h", h=8)
            ts("dve", cgov, cgv[:, 0:8, :], PAR[:, P_OWN:P_OWN + 1], None, ALU.mult, None, [bcg, bPAR], [bcgo])
            for r in range(1, 4):
                stt("dve", cgov, cgv[:, 8 * r:8 * r + 8, :], PAR[:, P_OWN + r:P_OWN + r + 1], cgov, ALU.mult, ALU.add,
                    [bcg, bPAR], [bcgo])
            biasov = biaso.rearrange("p (q k h) -> p q k h", q=2, h=8)
            for qc in range(2):
                tt("dve", biasov[:, qc], cref[:, qc * 8:(qc + 1) * 8].unsqueeze(1).to_broadcast([128, 8, 8]), cgov, ALU.subtract,
                   [bcref, bcgo], [bbiaso])
            kvs = [(sbb(1024), sbb(1024)) for _ in range(3)]
            ptl = [sbb(512) for _ in range(2)]
            rd2, brd2 = sf(512); o2, bo2 = sf(512); rs2, brs2 = sf(512); osq2, bosq2 = sbb(512)
            OT = [0, 2]; DEN = [1, 3]; SB_ = [4, 5]; NSB = 6
            ki = [0]; pi = [0]; si = [0]
            for h in range(H):
                started = [False, False]
                for srcr in range(5):
                    (tk, btk), (tv, btv) = kvs[ki[0] % 3]; ki[0] += 1
                    if srcr < 4:
                        ksrc = rcvK[h // 4][srcr * 512 + (h % 4) * 128:srcr * 512 + (h % 4 + 1) * 128, :]
                        vsrc = [rcvV[i][srcr * 512:(srcr + 1) * 512, h * 128:(h + 1) * 128] for i in range(2)]
                        rb = [bRCV]
                    else:
                        ksrc = sndK[h // 4][(h % 4) * 128:(h % 4 + 1) * 128, :]
                        vsrc = [sndV[i][:, h * 128:(h + 1) * 128] for i in range(2)]
                        rb = [bSND]
                    dma("sp", tk, ksrc, rb, [btk])
                    tv3 = tv.rearrange("p (k d) -> p k d", d=128)
                    for i in range(2):
                        dma("sp", tv3[:, 4 * i:4 * i + 4, :], vsrc[i].rearrange("(k p) d -> p k d", p=128), rb, [btv])
                    for kb in range(8):
                        for qc in range(2):
                            if srcr == 4 and qc == 0 and kb >= 4:
                                continue
                            sbk = SB_[si[0] % 2]; si[0] += 1
                            mm(PSB[sbk][:, :], [(tk[:, kb * 128:(kb + 1) * 128], QT(h)[:, qc * 512:(qc + 1) * 512])],
                               [btk, bREG[h]], [bPS[sbk]])
                            tp, btp = ptl[pi[0] % 2]; pi[0] += 1
                            if srcr < 4:
                                bap = biasv[:, qc, srcr * 8 + kb, h:h + 1]; bb = bbias
                            else:
                                bap = biasov[:, qc, kb, h:h + 1]; bb = bbiaso
                            act(tp, PSB[sbk][:, :], AF.Exp, [bPS[sbk], bb], [btp], scale=SCALE, bias=bap)
                            if srcr == 4 and (kb // 4 == qc):
                                tt("pool", tp, tp, DMB[:, kb % 4, :], ALU.mult, [btp, bC], [btp])
                            last = (srcr == 4 and ((qc == 0 and kb == 3) or (qc == 1 and kb == 7)))
                            first = not started[qc]; started[qc] = True

                            def pv(e, tv=tv, tp=tp, kb=kb, qc=qc, first=first, last=last):
                                e.matmul(PSB[OT[qc]][:, :], lhsT=tv[:, kb * 128:(kb + 1) * 128], rhs=tp, start=first, stop=last)
                                return e.matmul(PSB[DEN[qc]][:, :], lhsT=ONEB[:], rhs=tp, start=first, stop=last)
                            P.op("pe", pv, reads=[btv, btp, bC], writes=[bPS[OT[qc]], bPS[DEN[qc]]])
                for qc in range(2):
                    P.op("dve", lambda e, qc=qc, rd2=rd2: e.reciprocal(out=rd2, in_=PSB[DEN[qc]][:, :]), reads=[bPS[DEN[qc]]], writes=[brd2])
                    tt("dve", o2, PSB[OT[qc]][:, :], rd2, ALU.mult, [bPS[OT[qc]], brd2], [bo2])
                    act(osq2, o2, AF.Square, [bo2], [bosq2])
                    mm(PSB[NSB][:, :], [(ONEB[:], osq2)], [bosq2, bC], [bPS[NSB]])
                    act(rs2, PSB[NSB][:, :], AF.Ln, [bPS[NSB]], [brs2], scale=1.0 / DH, bias=EPS)
                    act(rs2, rs2, AF.Exp, [brs2], [brs2], scale=-0.5)
                    stt("dve", XN[:, h, qc * 512:(qc + 1) * 512], o2, PAR[:, po + PL_GA + h:po + PL_GA + h + 1], rs2,
                        ALU.mult, ALU.mult, [bo2, brs2, bPAR], [bXN[h]])
            barrier()

            def ev_res(ti, n, a, z, ps, bps):
                tt("dve", X[:, ti, a:z], ps, X[:, ti, a:z], ALU.add, [bps, bX[ti]], [bX[ti]])
            if o_dbg is not None and l == 0 and "DBGX1" in DBG:
                dg32, bdg32 = sf(16 * NS)
                cp("dve", dg32.rearrange("p (k t) -> p k t", t=NS), XN[:, :, NP:NT], bXN, [bdg32])
                dma("sp", o_dbg, dg32, [bdg32], [])
            proj_fm(w_out, l * D, [128 * i for i in range(16)], KT, rhs_xn, bXN, ev_res)
            if o_dbg is not None and l == 0 and "DBGX1" in DBG:
                dg33, bdg33 = sf(16 * NS)
                cp("dve", dg33.rearrange("p (k t) -> p k t", t=NS), X[:, :, NP:NT], bX, [bdg33])
                dma("sp", o_dbg2, dg33, [bdg33], [])
            barrier()

            rmsnorm(po + PL_GMLP, xn_out)
            Hh = lambda t: REG[:, t, 0:NT]
            r32 = [sf(344), sf(344)]
            ri = [0]
            for g in range(4):
                def ev_up(ti, n, a, z, ps, bps):
                    t_, b_ = r32[ri[0] % 2]; ri[0] += 1
                    act(t_[:, 0:z - a], ps, AF.Relu, [bps], [b_])
                    tt("pool", Hh(ti)[:, a:z], t_[:, 0:z - a], t_[:, 0:z - a], ALU.mult, [b_], [bREG[ti]])
                proj_fm(w_up, l * D, [g * 2048 + 128 * i for i in range(16)], KT, rhs_xn, bXN, ev_up)
                proj_fm(w_down, l * DFF + g * 2048, [128 * i for i in range(16)], 16,
                        lambda kt, a, z: Hh(kt)[:, a:z], bREG, ev_res)
            barrier()

            rmsnorm(po + PL_GPLE, xn_out)
            bWP = bf("WP")
            gs = [sf(344), sf(344)]
            gi2 = [0]
            gate = {}

            def ev_gate(ti, n, a, z, ps, bps):
                t_, b_ = gs[gi2[0] % 2]; gi2[0] += 1
                act(t_[:, 0:z - a], ps, AF.Sigmoid, [bps], [b_])
                if ti % 2 == 0 and n == 0:
                    dma("pool", WP[:], w_ple[l * PLE:(l + 1) * PLE, ti * 128:ti * 128 + 256].rearrange("(k p) c -> p k c", p=128),
                        [], [bWP])
                k = nextps()
                mm(PSB[k][:, 0:z - a], [(WP[:, kk, (ti % 2) * 128:(ti % 2) * 128 + 128], PTE[:, kk, a:z]) for kk in range(2)],
                   [bWP, bPTE], [bPS[k]])
                tt("dve", t_[:, 0:z - a], PSB[k][:, 0:z - a], t_[:, 0:z - a], ALU.mult, [bPS[k], b_], [b_])
                tt("dve", X[:, ti, a:z], X[:, ti, a:z], t_[:, 0:z - a], ALU.add, [bX[ti], b_], [bX[ti]])
            proj_fm(w_gate, l * D, [128 * i for i in range(16)], KT, rhs_xn, bXN, ev_gate)
            if o_dbg is not None and l == 0 and "DBGX" in DBG:
                dg32, bdg32 = sf(16 * NS)
                cp("dve", dg32.rearrange("p (k t) -> p k t", t=NS), X[:, :, NP:NT], bX, [bdg32])
                dma("sp", o_dbg, dg32, [bdg32], [])
            barrier()

        yt = [sf(NT), sf(NT)]

        def y_out(kt, rstd, b_r, g):
            t_, b_ = yt[kt % 2]
            stt("dve", t_, X[:, kt, :], g, rstd, ALU.mult, ALU.mult, [bX[kt], b_r, bPAR], [b_])
            dma("sp", o_y[kt * 128:(kt + 1) * 128, :], t_, [b_], [])
        rmsnorm(P_GFIN, y_out)
    try:
        body()
    except _Stop:
        pass
    P.emit()
    es.close()
    return nc


def _finish(P, es, nc):
    P.emit()
    es.close()
    return nc


_NC = None


def kernel(**inp):
    global _NC
    inp = {k: np.asarray(v) for k, v in inp.items()}
    if _NC is None:
        _NC = build_program()
    nc = _NC
    f32 = np.float32
    cst = make_consts()
    shared = {
        "cst": cst,
        "w_in": np.ascontiguousarray(inp["w_in"].reshape(L * D, INW)),
        "w_out": np.ascontiguousarray(inp["w_out"].reshape(L * D, D)),
        "w_up": np.ascontiguousarray(inp["w_up"].reshape(L * D, DFF)),
        "w_down": np.ascontiguousarray(inp["w_down"].reshape(L * DFF, D)),
        "w_gate": np.ascontiguousarray(inp["w_ple_gate"].reshape(L * D, D)),
        "w_ple": np.ascontiguousarray(inp["w_ple"].reshape(L * PLE, D)),
    }
    if "TINYW" in DBG:
        for k in ("w_out", "w_up", "w_down", "w_gate", "w_ple"):
            shared[k] = np.zeros((128, 128), f32)
    for l in range(L):
        for hh in range(2):
            shared["ck%d_%d" % (l, hh)] = np.ascontiguousarray(
                inp["cache_k"][l][:, :, 4 * hh:4 * hh + 4].transpose(0, 2, 1, 3)).reshape(NPHYS * 64, 1024)
            shared["cv%d_%d" % (l, hh)] = np.ascontiguousarray(
                inp["cache_v"][l][:, :, 4 * hh:4 * hh + 4].transpose(0, 2, 1, 3)).reshape(NPHYS * 64, 1024)
        shared["clf%d" % l] = np.ascontiguousarray(inp["cache_logf"][l].transpose(0, 2, 1)).reshape(NPHYS * 8, 128)
    in_maps = []
    for c in range(8):
        b, j = c // 4, c % 4
        rows = slice(1024 * j, 1024 * (j + 1))
        m = dict(shared)
        m["xT"] = np.ascontiguousarray(np.concatenate([inp["x_prompt"][b, rows], inp["x_sample"][c]], 0).T.astype(f32))
        m["pT"] = np.ascontiguousarray(np.concatenate(
            [np.concatenate([inp["p_prompt"][l, b, rows], inp["p_sample"][l, c]], 0).T for l in range(L)], 0).astype(f32))
        m["scv"] = np.ascontiguousarray(np.concatenate([inp["state_conv"][l, c].T for l in range(L)], 0).astype(f32))
        m["par"] = pack_params(inp, j)
        m["ptab"] = np.ascontiguousarray(inp["page_table"][c].reshape(128, 1).astype(np.int32))
        in_maps.append(m)
    res = run_bass_kernel_spmd(nc, in_maps, core_ids=list(range(8))).results
    global _LAST
    _LAST = res

    y_prompt = np.empty((2, 4096, D), f32); y_sample = np.empty((8, 8, D), f32)
    k_prompt = np.empty((L, 2, 4096, H, DH), f32); v_prompt = np.empty((L, 2, 4096, H, DH), f32)
    lf_prompt = np.empty((L, 2, 4096, H), f32); cv_prompt = np.empty((L, 2, HALO, CW), f32)
    k_sample = np.empty((L, 8, 8, H, DH), f32); v_sample = np.empty((L, 8, 8, H, DH), f32)
    lf_sample = np.empty((L, 8, 8, H), f32); cv_sample = np.empty((L, 8, HALO, CW), f32)
    for c in range(8):
        b, j = c // 4, c % 4
        rows = slice(1024 * j, 1024 * (j + 1))
        r = res[c]
        yT = r["o_y"]
        y_prompt[b, rows] = yT[:, :NP].T
        y_sample[c] = yT[:, NP:].T
        ok = r["o_k"].reshape(L, 1024, NT); ov = r["o_v"].reshape(L, NT, 1024)
        olf = r["o_lf"].reshape(L, NT, 8); ocv = r["o_cv"].reshape(L, CW, 2 * HALO)
        for l in range(L):
            k_prompt[l, b, rows] = ok[l][:, :NP].T.reshape(1024, H, DH)
            k_sample[l, c] = ok[l][:, NP:].T.reshape(8, H, DH)
            v_prompt[l, b, rows] = ov[l][:NP].reshape(1024, H, DH)
            v_sample[l, c] = ov[l][NP:].reshape(8, H, DH)
            lf_prompt[l, b, rows] = olf[l][:NP]
            lf_sample[l, c] = olf[l][NP:]
            if j == 3:
                cv_prompt[l, b] = ocv[l][:, :HALO].T
            cv_sample[l, c] = ocv[l][:, HALO:].T
    return (y_prompt, y_sample, k_prompt, v_prompt, lf_prompt, cv_prompt, k_sample, v_sample, lf_sample, cv_sample)
```

```python
import numpy as np
from contextlib import ExitStack
import concourse.bass as bass
import concourse.mybir as mybir
from concourse.bass_utils import run_bass_kernel_spmd

F32 = mybir.dt.float32
BF16 = mybir.dt.bfloat16
I32 = mybir.dt.int32
AF = mybir.ActivationFunctionType
ALU = mybir.AluOpType
AX = mybir.AxisListType

D = 2048; KT = 16; NP = 1024; NS = 8; NT = NP + NS
L = 2; H = 8; DH = 128; CW = 1024; CK = 31; HALO = CK - 1
DFF = 8192; PLE = 256; NPHYS = 1280; NPG = 128
INW = 3 * 1024 + 8 + 2 * CW
EPS = 1e-6
SCALE = DH ** -0.5
CH = [(0, 344), (344, 688), (688, 1032)]
NEG = -30000.0
RS = 2048 + HALO + 16
ENGS = ("pe", "act", "dve", "pool", "sp")
DMA_POOL = 12


class Buf:
    __slots__ = ("name", "last_w", "readers", "excl")

    def __init__(self, name):
        self.name = name
        self.last_w = None
        self.readers = {}
        self.excl = False


class Op:
    __slots__ = ("eng", "fn", "deps", "signal", "sigval", "kind", "dsem", "dval")


class Prog:
    def __init__(self, nc):
        self.nc = nc
        self.ops = {e: [] for e in ENGS}
        self.dma_count = {q: [0] * DMA_POOL for q in ("pool", "sp")}
        self.dma_rr = {"pool": 0, "sp": 0}
        self.dma_last = {q: [None] * DMA_POOL for q in ("pool", "sp")}
        self.cc_count = 0
        self.cc_last = None
        self.phase = Buf("phase")

    def op(self, eng, fn, reads=(), writes=(), kind="c", nophase=False):
        o = Op()
        o.eng = eng; o.fn = fn; o.signal = False; o.sigval = None
        o.kind = kind; o.dsem = None; o.dval = None
        deps = []
        writes = list(writes) + [b for b in reads if b.excl and b not in writes]
        reads = [b for b in reads if not b.excl]
        if not nophase:
            reads.append(self.phase)
        for b in reads:
            if b.last_w is not None:
                deps.append(b.last_w)
        for b in writes:
            if b.last_w is not None:
                deps.append(b.last_w)
            deps.extend(b.readers.values())
        if kind == "dma":
            k = self.dma_rr[eng]
            self.dma_rr[eng] = (k + 1) % DMA_POOL
            self.dma_count[eng][k] += 1
            o.dsem = (eng, k)
            o.dval = 16 * self.dma_count[eng][k]
            if self.dma_last[eng][k] is not None:
                deps.append(self.dma_last[eng][k])
            self.dma_last[eng][k] = o
        elif kind == "cc":
            self.cc_count += 1
            o.dsem = ("cc", 0)
            o.dval = self.cc_count
            if self.cc_last is not None:
                deps.append(self.cc_last)
            self.cc_last = o
        fl = []
        for d in deps:
            if d is o:
                continue
            if d.kind == "c" and d.eng == eng and eng == "pe":
                continue
            fl.append(d)
        o.deps = fl
        for d in fl:
            if d.kind == "c":
                d.signal = True
        for b in reads:
            key = eng if kind == "c" else ("x", id(o))
            b.readers[key] = o
        for b in writes:
            b.last_w = o
            b.readers = {}
        self.ops[eng].append(o)
        return o

    def barrier(self, fn):
        self.op("dve", fn, writes=[self.phase], nophase=True)

    def emit(self):
        nc = self.nc
        with ExitStack() as es:
            csem = {e: es.enter_context(nc.semaphore("c_" + e)) for e in ("pe", "act", "dve", "pool")}
            dsem = {}
            for q in ("pool", "sp"):
                for k in range(DMA_POOL):
                    if self.dma_count[q][k] > 0:
                        dsem[(q, k)] = es.enter_context(nc.semaphore("d_%s%d" % (q, k)))
            dsem[("cc", 0)] = es.enter_context(nc.semaphore("ccs"))
            for e in ENGS:
                c = 0
                for o in self.ops[e]:
                    if o.kind == "c" and o.signal:
                        c += 1
                        o.sigval = c
            block = es.enter_context(nc.Block())
            prog = self

            def run(ename, eobj):
                waited = {}
                for o in prog.ops[ename]:
                    need = {}
                    for d in o.deps:
                        if d.kind == "c":
                            key = ("c", d.eng); val = d.sigval
                        else:
                            key = ("d",) + d.dsem; val = d.dval
                        if need.get(key, 0) < val:
                            need[key] = val
                    for key, val in need.items():
                        if waited.get(key, 0) >= val:
                            continue
                        waited[key] = val
                        sem = dsem[key[1:]] if key[0] == "d" else csem[key[1]]
                        eobj.wait_ge(sem, val)
                    ins = o.fn(eobj)
                    if o.kind == "dma":
                        ins.then_inc(dsem[o.dsem], 16)
                    elif o.kind == "cc":
                        ins.then_inc(dsem[o.dsem])
                    elif o.signal:
                        ins.then_inc(csem[ename], 1)
                if ename in ("pool", "sp"):
                    for k in range(DMA_POOL):
                        if prog.dma_count[ename][k] > 0:
                            eobj.wait_ge(dsem[(ename, k)], 16 * prog.dma_count[ename][k])
                    if ename == "pool" and prog.cc_count:
                        eobj.wait_ge(dsem[("cc", 0)], prog.cc_count)

            @block.tensor
            def _(e):
                run("pe", e)

            @block.scalar
            def _(e):
                run("act", e)

            @block.vector
            def _(e):
                run("dve", e)

            @block.gpsimd
            def _(e):
                run("pool", e)

            @block.sync
            def _(e):
                run("sp", e)


C_ID = 0; C_TRI = 128; C_SUP = 256; C_UPS = 384; C_DM = 512; C_CAUS = 2560; C_OFF64 = 2568
C_OFF8 = 2632; C_ONE = 2640; NCST = 2768


def make_consts():
    c = np.zeros((128, NCST), np.float32)
    i = np.arange(128)
    c[:, C_ID:C_ID + 128] = np.eye(128)
    c[:, C_TRI:C_TRI + 128] = (i[:, None] <= i[None, :])
    c[:, C_SUP:C_SUP + 128] = (i[:, None] > i[None, :])
    c[:, C_UPS:C_UPS + 128] = (i[:, None] > i[None, :])
    f = np.arange(512)
    for k in range(4):
        c[:, C_DM + 512 * k:C_DM + 512 * (k + 1)] = (128 * k + i[:, None] <= f[None, :])
    c[:8, C_CAUS:C_CAUS + 8] = (np.arange(8)[:, None] <= np.arange(8)[None, :])
    c[:, C_OFF64:C_OFF64 + 64] = np.arange(64)[None, :]
    c[:, C_OFF8:C_OFF8 + 8] = np.arange(8)[None, :]
    c[:, C_ONE:C_ONE + 128] = 1.0
    return c


PL_GMIX = 0; PL_GMLP = 16; PL_GPLE = 32; PL_CW = 48; PL_CB = 296; PL_LG = 304; PL_LB = 312
PL_GA = 320; PL_BF = 328; PL_N = 336
P_GFIN = 2 * PL_N; P_PREV = P_GFIN + 16; P_OWN = P_PREV + 4; P_MB = P_OWN + 4; P_WQ = P_MB + 32
NPAR = P_WQ + 64


def pack_params(inp, j):
    p = np.zeros((128, NPAR), np.float32)
    fm = lambda v: np.ascontiguousarray(v.reshape(-1, 128).T)
    for l in range(L):
        o = l * PL_N
        p[:, o + PL_GMIX:o + PL_GMIX + 16] = fm(inp["g_mix"][l])
        p[:, o + PL_GMLP:o + PL_GMLP + 16] = fm(inp["g_mlp"][l])
        p[:, o + PL_GPLE:o + PL_GPLE + 16] = fm(inp["g_ple"][l])
        cw = inp["conv_w"][l]
        p[:, o + PL_CW:o + PL_CW + 248] = cw.reshape(CK, 8, 128).transpose(2, 1, 0).reshape(128, 248)
        p[:, o + PL_CB:o + PL_CB + 8] = fm(inp["conv_b"][l])
        p[:, o + PL_LG:o + PL_LG + 8] = fm(inp["conv_ln_g"][l])
        p[:, o + PL_LB:o + PL_LB + 8] = fm(inp["conv_ln_b"][l])
        p[:, o + PL_GA:o + PL_GA + 8] = fm(inp["g_attn_out"][l])
        p[:, o + PL_BF:o + PL_BF + 8] = inp["b_f"][l][None, :]
    p[:, P_GFIN:P_GFIN + 16] = fm(inp["g_final"])
    for r in range(4):
        p[:, P_PREV + r] = 1.0 if r == j - 1 else 0.0
        p[:, P_OWN + r] = 1.0 if r == j else 0.0
        p[:, P_MB + 8 * r:P_MB + 8 * r + 8] = 0.0 if r < j else NEG
    for qc in range(2):
        nblk = 8 * j + 4 * qc
        p[:, P_WQ + 32 * qc:P_WQ + 32 * qc + nblk] = 1.0
    return p


class _Stop(Exception):
    pass


KSTOP = None
import os as _os
DBG = set(_os.environ.get('KDBG', '').split(','))


def build_program():
    try:
        return _build_program()
    finally:
        pass


def _build_program():
    nc = bass.Bass("TRN2", target_bir_lowering=False, num_devices=8)
    dt_in = lambda n, s, d=F32: nc.dram_tensor(n, s, d, kind="ExternalInput").ap()
    dt_out = lambda n, s, d=F32: nc.dram_tensor(n, s, d, kind="ExternalOutput").ap()
    xT = dt_in("xT", [D, NT])
    pT = dt_in("pT", [L * PLE, NT])
    scv = dt_in("scv", [L * CW, HALO])
    par = dt_in("par", [128, NPAR])
    cst = dt_in("cst", [128, NCST])
    ptab = dt_in("ptab", [128, 1], I32)
    w_in = dt_in("w_in", [L * D, INW])
    if "TINYW" in DBG:
        w_out = dt_in("w_out", [128, 128]); w_up = dt_in("w_up", [128, 128]); w_down = dt_in("w_down", [128, 128])
        w_gate = dt_in("w_gate", [128, 128]); w_ple = dt_in("w_ple", [128, 128])
    else:
        w_out = dt_in("w_out", [L * D, D])
        w_up = dt_in("w_up", [L * D, DFF])
        w_down = dt_in("w_down", [L * DFF, D])
        w_gate = dt_in("w_gate", [L * D, D])
        w_ple = dt_in("w_ple", [L * PLE, D])
    ck = [[dt_in("ck%d_%d" % (l, hh), [NPHYS * 64, 1024]) for hh in range(2)] for l in range(L)]
    cv = [[dt_in("cv%d_%d" % (l, hh), [NPHYS * 64, 1024]) for hh in range(2)] for l in range(L)]
    clf = [dt_in("clf%d" % l, [NPHYS * 8, 128]) for l in range(L)]
    o_y = dt_out("o_y", [D, NT])
    o_k = dt_out("o_k", [L * 1024, NT])
    o_v = dt_out("o_v", [L * NT, 1024])
    o_lf = dt_out("o_lf", [L * NT, 8])
    o_cv = dt_out("o_cv", [L * CW, 2 * HALO])
    o_dbg = dt_out("o_dbg", [128, 16 * NS]) if "DBGOUT" in DBG else None
    o_dbg2 = dt_out("o_dbg2", [128, 16 * NS]) if "DBGOUT" in DBG else None
    sndK = [nc.dram_tensor("sndK%d" % i, [512, 1024], BF16, kind="Internal").ap() for i in range(2)]
    rcvK = [nc.dram_tensor("rcvK%d" % i, [4 * 512, 1024], BF16, kind="Internal").ap() for i in range(2)]
    sndV = [nc.dram_tensor("sndV%d" % i, [512, 1024], BF16, kind="Internal").ap() for i in range(2)]
    rcvV = [nc.dram_tensor("rcvV%d" % i, [4 * 512, 1024], BF16, kind="Internal").ap() for i in range(2)]
    sndM = nc.dram_tensor("sndM", [64, 1024], BF16, kind="Internal").ap()
    rcvM = nc.dram_tensor("rcvM", [4 * 64, 1024], BF16, kind="Internal").ap()
    LFR = 2048 + HALO
    sndf = sndM[32:48, :].bitcast(F32).rearrange("r (q h) -> (r q) h", h=8)
    rcvf = [rcvM[r * 64 + 32:r * 64 + 48, :].bitcast(F32).rearrange("r (q h) -> (r q) h", h=8) for r in range(4)]
    RG = [[0, 1, 2, 3], [4, 5, 6, 7]]

    P = Prog(nc)
    es = ExitStack()
    sb = lambda n, s, d: es.enter_context(nc.sbuf_tensor(n, s, d))
    X = sb("X", [128, KT, NT], F32)
    XN = sb("XN", [128, KT, NT], BF16)
    REG = sb("REG", [128, 16, NT + HALO], BF16)
    PAR = sb("PAR", [128, NPAR], F32)
    CF = sb("CF", [128, 512], F32)
    ONE32 = sb("ONE32", [128, 128], F32)
    IDB = sb("IDB", [128, 128], BF16)
    ONEB = sb("ONEB", [128, 128], BF16)
    DMB = sb("DMB", [128, 4, 512], BF16)
    CAUS = sb("CAUS", [128, 8], F32)
    OFF = sb("OFF", [128, 72], F32)
    WS = [sb("WS%d" % i, [128, 16, 256], BF16) for i in range(2)]
    WF = sb("WF", [128, 16, 8], BF16)
    WP = sb("WP", [128, 2, 256], BF16)
    PTE = sb("PTE", [128, 2, NT], BF16)
    KTS = sb("KTS", [128, H, NS], BF16)
    VTS = sb("VTS", [128, 1024], BF16)
    LF = sb("LF", [128, 9, 8], F32)
    US32 = sb("US32", [128, 8, HALO + NS], F32)
    USB = sb("USB", [128, 8, HALO + NS], BF16)
    CVO = sb("CVO", [128, 8, HALO], F32)
    SCF = sb("SCF", [128, 6400], F32)
    SCB = sb("SCB", [128, 7808], BF16)
    SCI = sb("SCI", [128, 144], I32)
    PSB = [es.enter_context(nc.psum_tensor("ps%d" % i, [128, 512], F32)) for i in range(7)]
    PST = es.enter_context(nc.psum_tensor("pst", [128, 1024], BF16))

    B = {}

    def bf(name):
        if name not in B:
            B[name] = Buf(name)
        return B[name]
    bX = [bf("X%d" % k) for k in range(KT)]
    bXN = [bf("XN%d" % k) for k in range(KT)]
    bREG = [bf("REG%d" % k) for k in range(16)]
    bPS = [bf("PS%d" % k) for k in range(7)]
    bPST = bf("PST")
    for b_ in bPS + [bPST]:
        b_.excl = True
    bWS = [bf("WS0"), bf("WS1")]
    wsi = [0]
    psi = [0]

    scr = {"f": 0, "b": 0, "n": 0}

    def sf(n):
        o = scr["f"]; scr["f"] += n
        assert scr["f"] <= 6400, scr
        return SCF[:, o:o + n], bf("sf@%d" % o)

    def sbb(n):
        o = scr["b"]; scr["b"] += n
        assert scr["b"] <= 7808, scr
        return SCB[:, o:o + n], bf("sb@%d" % o)

    stage = [0]

    def ckpt(name):
        if _os.environ.get("KCUT") == name:
            raise _Stop()

    def barrier():
        if o_dbg is not None and _os.environ.get("KDUMP") == str(stage[0] + 1):
            d1, bd1 = sf(16 * NS); d2, bd2 = sf(16 * NS)
            cp("dve", d1.rearrange("p (k t) -> p k t", t=NS), XN[:, :, NP:NT], bXN, [bd1])
            dma("sp", o_dbg, d1, [bd1], [])
            cp("dve", d2.rearrange("p (k t) -> p k t", t=NS), X[:, :, NP:NT], bX, [bd2])
            dma("sp", o_dbg2, d2, [bd2], [])
        P.barrier(lambda e: e.memset(SCI[:, 2:3], 0))
        scr["f"] = 0; scr["b"] = 0
        stage[0] += 1
        if KSTOP is not None and stage[0] >= KSTOP:
            raise _Stop()

    def dma(q, out, in_, reads, writes, **kw):
        P.op(q, lambda e: e.dma_start(out=out, in_=in_, **kw), reads=reads, writes=writes, kind="dma")

    def act(out, in_, func, reads, writes, **kw):
        P.op("act", lambda e: e.activation(out=out, in_=in_, func=func, **kw), reads=reads, writes=writes)

    def tt(eng, out, in0, in1, op, reads, writes):
        P.op(eng, lambda e: e.tensor_tensor(out=out, in0=in0, in1=in1, op=op), reads=reads, writes=writes)

    def ts(eng, out, in0, s1, s2, op0, op1, reads, writes):
        if op1 is None:
            P.op(eng, lambda e: e.tensor_scalar(out=out, in0=in0, scalar1=s1, scalar2=None, op0=op0),
                 reads=reads, writes=writes)
        else:
            P.op(eng, lambda e: e.tensor_scalar(out=out, in0=in0, scalar1=s1, scalar2=s2, op0=op0, op1=op1),
                 reads=reads, writes=writes)

    def stt(eng, out, in0, scalar, in1, op0, op1, reads, writes):
        P.op(eng, lambda e: e.scalar_tensor_tensor(out=out, in0=in0, scalar=scalar, in1=in1, op0=op0, op1=op1),
             reads=reads, writes=writes)

    def cp(eng, out, in_, reads, writes):
        P.op(eng, lambda e: e.tensor_copy(out=out, in_=in_), reads=reads, writes=writes)

    def mm(out, pairs, reads, writes, start=True, stop=True):
        def fn(e):
            ins = None
            n = len(pairs)
            for i, (l_, r_) in enumerate(pairs):
                ins = e.matmul(out, lhsT=l_, rhs=r_, start=(start and i == 0), stop=(stop and i == n - 1))
            return ins
        P.op("pe", fn, reads=reads, writes=writes)

    def nextps():
        k = psi[0]; psi[0] = (k + 1) % 4
        return k

    def body():
        bPAR = bf("PAR"); bC = bf("CONST")
        dma("sp", PAR[:], par, [], [bPAR])
        t_c, b_c = sf(NCST)
        dma("sp", t_c, cst, [], [b_c])
        cp("dve", CF[:], t_c[:, 0:512], [b_c], [bC])
        cp("dve", ONE32[:], t_c[:, C_ONE:C_ONE + 128], [b_c], [bC])
        cp("dve", IDB[:], t_c[:, C_ID:C_ID + 128], [b_c], [bC])
        cp("dve", ONEB[:], t_c[:, C_ONE:C_ONE + 128], [b_c], [bC])
        cp("dve", DMB[:].rearrange("p a f -> p (a f)"), t_c[:, C_DM:C_DM + 2048], [b_c], [bC])
        cp("dve", CAUS[:], t_c[:, C_CAUS:C_CAUS + 8], [b_c], [bC])
        cp("dve", OFF[:], t_c[:, C_OFF64:C_OFF64 + 72], [b_c], [bC])
        for kt in range(KT):
            dma("sp", X[:, kt, :], xT[kt * 128:(kt + 1) * 128, :], [], [bX[kt]])
        bIDX = bf("IDX")
        dma("sp", SCI[:, 0:1], ptab, [], [bIDX])
        t_pf, b_pf = sf(144)
        offs, b_offs = sf(136)
        cp("dve", t_pf[:, 0:1], SCI[:, 0:1], [bIDX], [b_pf])
        cp("dve", offs[:, 0:64], OFF[:, 0:64], [bC], [b_offs])
        ts("dve", offs[:, 64:128], OFF[:, 0:64], 64.0, None, ALU.add, None, [bC], [b_offs])
        cp("dve", offs[:, 128:136], OFF[:, 64:72], [bC], [b_offs])
        ts("dve", t_pf[:, 1:2], t_pf[:, 0:1], 64.0, None, ALU.mult, None, [b_pf], [b_pf])
        ts("dve", t_pf[:, 8:72], offs[:, 0:64], t_pf[:, 1:2], None, ALU.add, None, [b_offs, b_pf], [b_pf])
        ts("dve", t_pf[:, 72:136], offs[:, 0:64], t_pf[:, 1:2], None, ALU.add, None, [b_offs, b_pf], [b_pf])
        ts("dve", t_pf[:, 2:3], t_pf[:, 0:1], 8.0, None, ALU.mult, None, [b_pf], [b_pf])
        ts("dve", t_pf[:, 136:144], offs[:, 128:136], t_pf[:, 2:3], None, ALU.add, None, [b_offs, b_pf], [b_pf])
        cp("dve", SCI[:, 8:144], t_pf[:, 8:144], [b_pf], [bIDX])
        IDXK = lambda h, sq: SCI[:, 8 + h * 16 + sq:8 + h * 16 + sq + 1]
        IDXL = lambda h: SCI[:, 136 + h:137 + h]
        barrier()

        def rmsnorm(gcol, out_fn):
            sq = [sbb(NT), sbb(NT)]
            pss = [4, 5, 6]
            for kt in range(KT):
                t_, b_ = sq[kt % 2]
                act(t_, X[:, kt, :], AF.Square, [bX[kt]], [b_])
                for n, (a, z) in enumerate(CH):
                    mm(PSB[pss[n]][:, 0:z - a], [(ONEB[:], t_[:, a:z])], [b_, bC], [bPS[pss[n]]],
                       start=(kt == 0), stop=(kt == KT - 1))
            rstd, b_r = sf(NT)
            for n, (a, z) in enumerate(CH):
                act(rstd[:, a:z], PSB[pss[n]][:, 0:z - a], AF.Ln, [bPS[pss[n]]], [b_r], scale=1.0 / D, bias=EPS)
            act(rstd, rstd, AF.Exp, [b_r], [b_r], scale=-0.5)
            for kt in range(KT):
                out_fn(kt, rstd, b_r, PAR[:, gcol + kt:gcol + kt + 1])

        def xn_out(kt, rstd, b_r, g):
            stt("dve", XN[:, kt, :], X[:, kt, :], g, rstd, ALU.mult, ALU.mult, [bX[kt], b_r, bPAR], [bXN[kt]])

        def load_w(slot, wap, row0, nkt, c0, ncols, dst_c0=0):
            src = wap[row0:row0 + nkt * 128, c0:c0 + ncols].rearrange("(k p) c -> p k c", p=128)
            dma("pool", WS[slot][:, 0:nkt, dst_c0:dst_c0 + ncols], src, [], [bWS[slot]])

        def proj_fm(wap, row0, col_tiles, nkt, rhs, rhs_bufs, evac):
            for t0 in range(0, len(col_tiles), 2):
                grp = col_tiles[t0:t0 + 2]
                slot = wsi[0]; wsi[0] ^= 1
                for gi, c0 in enumerate(grp):
                    load_w(slot, wap, row0, nkt, c0, 128, dst_c0=128 * gi)
                for gi, c0 in enumerate(grp):
                    for n, (a, z) in enumerate(CH):
                        k = nextps()
                        mm(PSB[k][:, 0:z - a],
                           [(WS[slot][:, kt, 128 * gi:128 * gi + 128], rhs(kt, a, z)) for kt in range(nkt)],
                           [bWS[slot]] + rhs_bufs, [bPS[k]])
                        evac(t0 + gi, n, a, z, PSB[k][:, 0:z - a], bPS[k])

        for l in range(L):
            po = l * PL_N
            bPTE = bf("PTE")
            pTl = pT[l * PLE:(l + 1) * PLE, :].rearrange("(k p) n -> p k n", p=128)
            for c0, c1 in ((0, 512), (512, NP), (NP, NT)):
                dma("pool", PTE[:, :, c0:c1], pTl[:, :, c0:c1], [], [bPTE])
            bUS = bf("US")
            dma("sp", US32[:, :, 0:HALO], scv[l * CW:(l + 1) * CW, :].rearrange("(c p) t -> p c t", p=128), [], [bUS])
            rmsnorm(po + PL_GMIX, xn_out)
            ckpt("A1")
            rhs_xn = lambda kt, a, z: XN[:, kt, a:z]
            QT = lambda h: REG[:, h, 0:NT]
            U = lambda c: REG[:, 8 + c, :]
            bCVO = bf("CVO")
            glu = {}

            def ev_conv(ti, n, a, z, ps, bps):
                c, isg = ti // 2, ti % 2
                if not isg:
                    v32, bv = sf(344)
                    cp("dve", v32[:, 0:z - a], ps, [bps], [bv])
                    glu[(c, n)] = (v32, bv)
                else:
                    v32, bv = glu[(c, n)]
                    sg, bs = sf(344)
                    act(sg[:, 0:z - a], ps, AF.Sigmoid, [bps], [bs])
                    tt("dve", v32[:, 0:z - a], v32[:, 0:z - a], sg[:, 0:z - a], ALU.mult, [bv, bs], [bv])
                    cp("dve", U(c)[:, HALO + a:HALO + z], v32[:, 0:z - a], [bv], [bREG[8 + c]])
                    if n == 2:
                        cp("dve", CVO[:, c, :], v32[:, 994 - a:1024 - a], [bv], [bCVO])
                        cp("dve", US32[:, c, HALO:HALO + NS], v32[:, 1024 - a:1032 - a], [bv], [bUS])
                    if (c * 3 + n) % 4 == 3:
                        scr["f"] -= 0
            base_f = scr["f"]
            ctiles = []
            for c in range(8):
                ctiles += [3080 + 128 * c, 3080 + 1024 + 128 * c]
            for c in range(8):
                scr["f"] = base_f + (c % 2) * 6 * 344
                proj_fm(w_in, l * D, ctiles[2 * c:2 * c + 2], KT, rhs_xn, bXN,
                        lambda ti, n, a, z, ps, bps, c=c: ev_conv(2 * c + ti, n, a, z, ps, bps))
            scr["f"] = base_f + 12 * 344
            ckpt("A2")
            cp("dve", USB[:], US32[:], [bUS], [bUS])
            dma("sp", o_cv[l * CW:(l + 1) * CW, 0:HALO].rearrange("(c p) t -> p c t", p=128), CVO[:], [bCVO], [])
            dma("sp", o_cv[l * CW:(l + 1) * CW, HALO:2 * HALO].rearrange("(c p) t -> p c t", p=128),
                US32[:, :, NS:NS + HALO], [bUS], [])
            bSND = bf("SND")
            for c in range(8):
                dst = sndM[0:HALO, :].rearrange("r c -> (r c)").rearrange("(c p t) -> p c t", p=128, t=HALO)
                dma("sp", dst[:, c, :], U(c)[:, HALO + 994:HALO + 1024], [bREG[8 + c]], [bSND])

            ckpt("A3")
            bKTS = bf("KTS")
            kb16 = [sbb(NT), sbb(NT)]
            k32 = [sf(344), sf(344)]
            cnt = [0]

            def ev_qk(ti, n, a, z, ps, bps):
                if ti < 8:
                    act(QT(ti)[:, a:z], ps, AF.Identity, [bps], [bREG[ti]])
                else:
                    h = ti - 8
                    t16, b16 = kb16[h % 2]
                    act(t16[:, a:z], ps, AF.Identity, [bps], [b16])
                    t32, b32 = k32[cnt[0] % 2]; cnt[0] += 1
                    cp("dve", t32[:, 0:z - a], ps, [bps], [b32])
                    if "NOKOUT" not in DBG:
                        dma("sp", o_k[l * 1024 + h * 128:l * 1024 + (h + 1) * 128, a:z], t32[:, 0:z - a], [b32], [])
                    if n == 2:
                        if "NOSNDK" not in DBG:
                            dma("sp", sndK[h // 4][(h % 4) * 128:(h % 4 + 1) * 128, :], t16[:, 0:NP], [b16], [bSND])
                        if "NOKTS" not in DBG:
                            cp("dve", KTS[:, h, :], t16[:, NP:NT], [b16], [bKTS])
            proj_fm(w_in, l * D, [128 * i for i in range(8 if "QONLY" in DBG else 16)], KT, rhs_xn, bXN, ev_qk)

            ckpt("A4")
            bWF = bf("WF"); bLF = bf("LF"); bVTS = bf("VTS")
            dma("pool", WF[:], w_in[l * D:(l + 1) * D, 3072:3080].rearrange("(k p) c -> p k c", p=128), [], [bWF])
            v32 = [sf(256), sf(256)]
            v16 = [sbb(256), sbb(256)]
            cnt2 = [0]
            for vs in range(4):
                slot = wsi[0]; wsi[0] ^= 1
                load_w(slot, w_in, l * D, KT, 2048 + 256 * vs, 128)
                load_w(slot, w_in, l * D, KT, 2048 + 256 * vs + 128, 128, dst_c0=128)
                for tb in range(9):
                    m = 128 if tb < 8 else NS
                    k = nextps()
                    mm(PSB[k][0:m, 0:256], [(XN[:, kt, tb * 128:tb * 128 + m], WS[slot][:, kt, :]) for kt in range(KT)],
                       [bWS[slot]] + bXN, [bPS[k]])
                    t32, b32 = v32[cnt2[0] % 2]; t16, b16 = v16[cnt2[0] % 2]; cnt2[0] += 1
                    cp("dve", t32[0:m, :], PSB[k][0:m, 0:256], [bPS[k]], [b32])
                    dma("sp", o_v[l * NT + tb * 128:l * NT + tb * 128 + m, 256 * vs:256 * (vs + 1)], t32[0:m, :], [b32], [])
                    if tb < 8:
                        act(t16[0:m, :], PSB[k][0:m, 0:256], AF.Identity, [bPS[k]], [b16])
                        dma("sp", sndV[tb // 4][(tb % 4) * 128:(tb % 4 + 1) * 128, 256 * vs:256 * (vs + 1)], t16[:, :], [b16], [bSND])
                    else:
                        act(VTS[0:m, 256 * vs:256 * (vs + 1)], PSB[k][0:m, 0:256], AF.Identity, [bPS[k]], [bVTS])
            for tb in range(9):
                m = 128 if tb < 8 else NS
                k = nextps()
                mm(PSB[k][0:m, 0:8], [(XN[:, kt, tb * 128:tb * 128 + m], WF[:, kt, :]) for kt in range(KT)],
                   [bWF] + bXN, [bPS[k]])
                tt("dve", LF[0:m, tb, :], PSB[k][0:m, 0:8], PAR[0:m, po + PL_BF:po + PL_BF + 8], ALU.add, [bPS[k], bPAR], [bLF])
            ckpt("A5")
            act(LF[:, 0:8, :], LF[:, 0:8, :], AF.Exp, [bLF], [bLF], scale=-1.0)
            act(LF[0:NS, 8, :], LF[0:NS, 8, :], AF.Exp, [bLF], [bLF], scale=-1.0)
            act(LF[:, 0:8, :], LF[:, 0:8, :], AF.Ln, [bLF], [bLF], bias=1.0)
            act(LF[0:NS, 8, :], LF[0:NS, 8, :], AF.Ln, [bLF], [bLF], bias=1.0)
            ts("dve", LF[:, 0:8, :], LF[:, 0:8, :], -1.0, None, ALU.mult, None, [bLF], [bLF])
            ts("dve", LF[0:NS, 8, :], LF[0:NS, 8, :], -1.0, None, ALU.mult, None, [bLF], [bLF])
            bSNDF = bf("SNDF")
            dma("sp", o_lf[l * NT:l * NT + NP, :].rearrange("(k p) h -> p k h", p=128), LF[:, 0:8, :], [bLF], [])
            dma("sp", o_lf[l * NT + NP:(l + 1) * NT, :], LF[0:NS, 8, :], [bLF], [])
            dma("sp", sndf.rearrange("(k p) h -> p k h", p=128), LF[:, 0:8, :], [bLF], [bSND])
            ckpt("A6")
            bRCV = bf("RCV"); bRCVF = bf("RCVF")
            if "NOCC" not in DBG:
                for s_, r_ in [(sndM, rcvM), (sndK[0], rcvK[0]), (sndK[1], rcvK[1]), (sndV[0], rcvV[0]), (sndV[1], rcvV[1])]:
                    P.op("pool", lambda e, s_=s_, r_=r_: e.collective_compute("AllGather", ALU.bypass, replica_groups=RG,
                         ins=[s_], outs=[r_]), reads=[bSND], writes=[bRCV], kind="cc")
            barrier()

            kg = [sbb(1024), sbb(1024), sbb(1024)]
            vg = [sbb(1024), sbb(1024), sbb(1024)]
            ktt, bktt = sbb(1024)
            pts, bpts = sbb(64)
            lg, blg = sf(128); lgt, blgt = sf(128); tb_, btb = sf(128); er, ber = sf(128)
            pex, bpex = sf(128)
            ncn, bncn = sf(8); ptn, bptn = sbb(8)
            tsum, btsum = sf(1)
            OTS, DENS, SP, RP = 0, 1, 2, 3
            mm(PSB[RP][0:NS, 0:8], [(CF[0:NS, C_TRI:C_TRI + NS], LF[0:NS, 8, :])], [bC, bLF], [bPS[RP]])
            ts("dve", ncn[0:NS, :], PSB[RP][0:NS, 0:8], -1.0, None, ALU.mult, None, [bPS[RP]], [bncn])
            gi = [0]
            for h in range(H):
                P.op("pool", lambda e, h=h, l=l, lg=lg: e.indirect_dma_start(out=lg, out_offset=None, in_=clf[l],
                     in_offset=bass.IndirectOffsetOnAxis(ap=IDXL(h), axis=0)), reads=[bIDX], writes=[blg], kind="dma")
                P.op("pe", lambda e, lg=lg: e.transpose(PSB[RP][:, 0:128], lg, CF[:, C_ID:C_ID + 128]), reads=[blg, bC], writes=[bPS[RP]])
                cp("dve", lgt, PSB[RP][:, 0:128], [bPS[RP]], [blgt])
                P.op("dve", lambda e, tsum=tsum, lg=lg: e.tensor_reduce(out=tsum, in_=lg, axis=AX.X, op=ALU.add), reads=[blg], writes=[btsum])
                ts("dve", tb_, ONE32[:], tsum[:, 0:1], None, ALU.mult, None, [bC, btsum], [btb])
                mm(PSB[RP][:, 0:128], [(lgt, CF[:, C_UPS:C_UPS + 128]), (CF[:, C_SUP:C_SUP + 128], tb_)],
                   [blgt, btb, bC], [bPS[RP]])
                act(er, PSB[RP][:, 0:128], AF.Exp, [bPS[RP]], [ber])
                for sq in range(16):
                    tk, btk = kg[gi[0] % 3]; tv, btv = vg[gi[0] % 3]; gi[0] += 1
                    P.op("pool", lambda e, tk=tk, h=h, sq=sq, l=l: e.indirect_dma_start(out=tk, out_offset=None, in_=ck[l][h // 4],
                         in_offset=bass.IndirectOffsetOnAxis(ap=IDXK(h, sq), axis=0)), reads=[bIDX], writes=[btk], kind="dma")
                    P.op("pool", lambda e, tv=tv, h=h, sq=sq, l=l: e.indirect_dma_start(out=tv, out_offset=None, in_=cv[l][h // 4],
                         in_offset=bass.IndirectOffsetOnAxis(ap=IDXK(h, sq), axis=0)), reads=[bIDX], writes=[btv], kind="dma")

                    def tfn(e, tk=tk):
                        ins = None
                        for s in range(8):
                            ins = e.transpose(PST[:, s * 128:(s + 1) * 128], tk[:, s * 128:(s + 1) * 128], IDB[:])
                        return ins
                    P.op("pe", tfn, reads=[btk, bC], writes=[bPST])
                    act(ktt, PST[:, :], AF.Identity, [bPST], [bktt])

                    def sfn(e, h=h, ktt=ktt, QT=QT):
                        ins = None
                        for s in range(8):
                            ins = e.matmul(PSB[SP][:, s * 8:(s + 1) * 8], lhsT=ktt[:, s * 128:(s + 1) * 128],
                                           rhs=QT(h)[:, NP:NT], start=True, stop=True)
                        return ins
                    P.op("pe", sfn, reads=[bktt, bREG[h]], writes=[bPS[SP]])
                    act(pex[:, 0:64], PSB[SP][:, 0:64], AF.Exp, [bPS[SP]], [bpex], scale=SCALE)
                    tt("dve", pts.rearrange("p (s t) -> p s t", t=NS), pex[:, 0:64].rearrange("p (s t) -> p s t", t=NS),
                       er[:, sq * 8:(sq + 1) * 8].unsqueeze(2).to_broadcast([128, 8, NS]), ALU.mult, [bpex, ber], [bpts])

                    def pvfn(e, h=h, sq=sq, tv=tv, pts=pts):
                        ins = None
                        for s in range(8):
                            st = (sq == 0 and s == 0)
                            e.matmul(PSB[OTS][:, h * 8:(h + 1) * 8], lhsT=tv[:, s * 128:(s + 1) * 128],
                                     rhs=pts[:, s * 8:(s + 1) * 8], start=st, stop=False)
                            ins = e.matmul(PSB[DENS][:, h * 8:(h + 1) * 8], lhsT=ONEB[:],
                                           rhs=pts[:, s * 8:(s + 1) * 8], start=st, stop=False)
                        return ins
                    P.op("pe", pvfn, reads=[btv, bpts, bC], writes=[bPS[OTS], bPS[DENS]])
                mm(PSB[SP][0:NS, 0:NS], [(KTS[:, h, :], QT(h)[:, NP:NT])], [bKTS, bREG[h]], [bPS[SP]])
                act(pex[0:NS, 0:NS], PSB[SP][0:NS, 0:NS], AF.Exp, [bPS[SP], bncn], [bpex], scale=SCALE, bias=ncn[0:NS, h:h + 1])
                tt("dve", ptn[0:NS, :], pex[0:NS, 0:NS], CAUS[0:NS, :], ALU.mult, [bpex, bC], [bptn])

                def nfn(e, h=h, ptn=ptn):
                    e.matmul(PSB[OTS][:, h * 8:(h + 1) * 8], lhsT=VTS[0:NS, h * 128:(h + 1) * 128], rhs=ptn[0:NS, :],
                             start=False, stop=True)
                    return e.matmul(PSB[DENS][:, h * 8:(h + 1) * 8], lhsT=ONEB[0:NS, :], rhs=ptn[0:NS, :],
                                    start=False, stop=True)
                P.op("pe", nfn, reads=[bVTS, bptn, bC], writes=[bPS[OTS], bPS[DENS]])
            rd, brd = sf(64); o32, bo32 = sf(64); osq, bosq = sbb(64); rs_, brs = sf(64)
            P.op("dve", lambda e, rd=rd: e.reciprocal(out=rd, in_=PSB[DENS][:, 0:64]), reads=[bPS[DENS]], writes=[brd])
            tt("dve", o32, PSB[OTS][:, 0:64], rd, ALU.mult, [bPS[OTS], brd], [bo32])
            act(osq, o32, AF.Square, [bo32], [bosq])
            mm(PSB[RP][:, 0:64], [(ONEB[:], osq)], [bosq, bC], [bPS[RP]])
            act(rs_, PSB[RP][:, 0:64], AF.Ln, [bPS[RP]], [brs], scale=1.0 / DH, bias=EPS)
            act(rs_, rs_, AF.Exp, [brs], [brs], scale=-0.5)
            tt("dve", o32, o32, rs_, ALU.mult, [bo32, brs], [bo32])
            tt("dve", XN[:, 0:8, NP:NT], o32.rearrange("p (h t) -> p h t", t=NS),
               PAR[:, po + PL_GA:po + PL_GA + 8].unsqueeze(2).to_broadcast([128, 8, NS]), ALU.mult,
               [bo32, bPAR], bXN[0:8])
            barrier()

            hal, bhal = sbb(4 * 8 * HALO)
            halv = hal.rearrange("p (r c t) -> p r c t", r=4, c=8)
            for r in range(4):
                src = rcvM[r * 64:r * 64 + HALO, :].rearrange("r c -> (r c)").rearrange("(c p t) -> p c t", p=128, t=HALO)
                dma("sp", halv[:, r], src, [bRCV], [bhal])
            for c in range(8):
                ts("dve", U(c)[:, 0:HALO], halv[:, 0, c, :], PAR[:, P_PREV:P_PREV + 1], None, ALU.mult, None, [bhal, bPAR], [bREG[8 + c]])
                for r in range(1, 4):
                    stt("dve", U(c)[:, 0:HALO], halv[:, r, c, :], PAR[:, P_PREV + r:P_PREV + r + 1], U(c)[:, 0:HALO],
                        ALU.mult, ALU.add, [bhal, bPAR], [bREG[8 + c]])
            dg = [sbb(128) for _ in range(4)]
            di = [0]
            for c in range(8):
                for j_ in range(CK):
                    td, bd = dg[di[0] % 4]; di[0] += 1
                    ts("pool", td, IDB[:], PAR[:, po + PL_CW + c * CK + j_:po + PL_CW + c * CK + j_ + 1], None, ALU.mult, None,
                       [bC, bPAR], [bd])

                    def cfn(e, td=td, c=c, j_=j_, U=U):
                        e.matmul(PSB[0][:, :], lhsT=td, rhs=U(c)[:, j_:j_ + 512], start=(j_ == 0), stop=(j_ == CK - 1))
                        e.matmul(PSB[1][:, :], lhsT=td, rhs=U(c)[:, 512 + j_:512 + j_ + 512], start=(j_ == 0), stop=(j_ == CK - 1))
                        return e.matmul(PSB[2][:, 0:NS], lhsT=td, rhs=USB[:, c, j_:j_ + NS], start=(j_ == 0), stop=(j_ == CK - 1))
                    P.op("pe", cfn, reads=[bd, bREG[8 + c], bUS], writes=[bPS[0], bPS[1], bPS[2]])
                cb = PAR[:, po + PL_CB + c:po + PL_CB + c + 1]
                act(XN[:, 8 + c, 0:512], PSB[0][:, :], AF.Identity, [bPS[0], bPAR], [bXN[8 + c]], bias=cb)
                act(XN[:, 8 + c, 512:1024], PSB[1][:, :], AF.Identity, [bPS[1], bPAR], [bXN[8 + c]], bias=cb)
                act(XN[:, 8 + c, NP:NT], PSB[2][:, 0:NS], AF.Identity, [bPS[2], bPAR], [bXN[8 + c]], bias=cb)
            sq2 = [sbb(NT), sbb(NT)]
            for c in range(8):
                t_, b_ = sq2[c % 2]
                act(t_, XN[:, 8 + c, :], AF.Square, [bXN[8 + c]], [b_])
                for n, (a, z) in enumerate(CH):
                    mm(PSB[n][:, 0:z - a], [(ONEB[:], XN[:, 8 + c, a:z])], [bXN[8 + c], bC], [bPS[n]], start=(c == 0), stop=(c == 7))
                    mm(PSB[3 + n][:, 0:z - a], [(ONEB[:], t_[:, a:z])], [b_, bC], [bPS[3 + n]], start=(c == 0), stop=(c == 7))
            mean, bmean = sf(NT); rstd2, brstd2 = sf(NT); tmp, btmp = sf(NT); sg2, bsg2 = sf(NT)
            for n, (a, z) in enumerate(CH):
                ts("dve", mean[:, a:z], PSB[n][:, 0:z - a], 1.0 / CW, None, ALU.mult, None, [bPS[n]], [bmean])
                tt("dve", tmp[:, a:z], mean[:, a:z], mean[:, a:z], ALU.mult, [bmean], [btmp])
                stt("dve", rstd2[:, a:z], PSB[3 + n][:, 0:z - a], 1.0 / CW, tmp[:, a:z], ALU.mult, ALU.subtract,
                    [bPS[3 + n], btmp], [brstd2])
            act(rstd2, rstd2, AF.Ln, [brstd2], [brstd2], bias=EPS)
            act(rstd2, rstd2, AF.Exp, [brstd2], [brstd2], scale=-0.5)
            for c in range(8):
                tt("dve", tmp, XN[:, 8 + c, :], mean, ALU.subtract, [bXN[8 + c], bmean], [btmp])
                tt("dve", tmp, tmp, rstd2, ALU.mult, [btmp, brstd2], [btmp])
                ts("dve", tmp, tmp, PAR[:, po + PL_LG + c:po + PL_LG + c + 1], PAR[:, po + PL_LB + c:po + PL_LB + c + 1],
                   ALU.mult, ALU.add, [btmp, bPAR], [btmp])
                act(sg2, tmp, AF.Sigmoid, [btmp], [bsg2])
                tt("dve", XN[:, 8 + c, :], tmp, sg2, ALU.mult, [btmp, bsg2], [bXN[8 + c]])
            barrier()

            if o_dbg is not None and l == 0 and "DBGX" not in DBG and "DBGX1" not in DBG:
                dg32, bdg32 = sf(16 * NS)
                cp("dve", dg32.rearrange("p (k t) -> p k t", t=NS), XN[:, :, NP:NT], bXN, [bdg32])
                dma("sp", o_dbg, dg32, [bdg32], [])
            lfa, blfa = sf(256); tot, btot = sf(256); cg, bcg = sf(256); car, bcar = sf(256)
            cref, bcref = sf(16); bias, bbias = sf(512); cgo, bcgo = sf(64); biaso, bbiaso = sf(128); wt, bwt = sf(256)
            lfv = lfa.rearrange("p (k h) -> p k h", h=8)
            for r in range(4):
                dma("sp", lfv[:, 8 * r:8 * r + 8, :], rcvf[r].rearrange("(k p) h -> p k h", p=128), [bRCV], [blfa])
            for blk in range(32):
                mm(PSB[4][:, blk * 8:(blk + 1) * 8], [(ONE32[:], lfv[:, blk, :])], [blfa, bC], [bPS[4]])
                mm(PSB[5][:, blk * 8:(blk + 1) * 8], [(CF[:, C_TRI:C_TRI + 128], lfv[:, blk, :])], [blfa, bC], [bPS[5]])
            cp("dve", tot, PSB[4][:, 0:256], [bPS[4]], [btot])
            P.op("dve", lambda e, car=car: e.memset(car[:, 0:8], 0.0), writes=[bcar])
            for blk in range(1, 32):
                tt("dve", car[:, blk * 8:(blk + 1) * 8], car[:, (blk - 1) * 8:blk * 8], tot[:, (blk - 1) * 8:blk * 8], ALU.add,
                   [bcar, btot], [bcar])
            tt("dve", cg, PSB[5][:, 0:256], car, ALU.add, [bPS[5], bcar], [bcg])
            for qc in range(2):
                tt("dve", wt.rearrange("p (h k) -> p h k", h=8), tot.rearrange("p (k h) -> p h k", h=8),
                   PAR[:, P_WQ + 32 * qc:P_WQ + 32 * (qc + 1)].unsqueeze(1).to_broadcast([128, 8, 32]), ALU.mult,
                   [btot, bPAR], [bwt])
                P.op("dve", lambda e, qc=qc, cref=cref, wt=wt: e.tensor_reduce(out=cref[:, qc * 8:(qc + 1) * 8], in_=wt.rearrange("p (h k) -> p h k", h=8),
                     axis=AX.X, op=ALU.add), reads=[bwt], writes=[bcref])
            biasv = bias.rearrange("p (q k h) -> p q k h", q=2, h=8)
            cgv = cg.rearrange("p (k h) -> p k h", h=8)
            for qc in range(2):
                tt("dve", biasv[:, qc], cref[:, qc * 8:(qc + 1) * 8].unsqueeze(1).to_broadcast([128, 32, 8]), cgv, ALU.subtract,
                   [bcref, bcg], [bbias])
                tt("dve", biasv[:, qc], biasv[:, qc], PAR[:, P_MB:P_MB + 32].unsqueeze(2).to_broadcast([128, 32, 8]), ALU.add,
                   [bbias, bPAR], [bbias])
            cgov = cgo.rearrange("p (k h) -> p k h", h=8)
            ts("dve", cgov, cgv[:, 0:8, :], PAR[:, P_OWN:P_OWN + 1], None, ALU.mult, None, [bcg, bPAR], [bcgo])
            for r in range(1, 4):
                stt("dve", cgov, cgv[:, 8 * r:8 * r + 8, :], PAR[:, P_OWN + r:P_OWN + r + 1], cgov, ALU.mult, ALU.add,
                    [bcg, bPAR], [bcgo])
            biasov = biaso.rearrange("p (q k h) -> p q k h", q=2, h=8)
            for qc in range(2):
                tt("dve", biasov[:, qc], cref[:, qc * 8:(qc + 1) * 8].unsqueeze(1).to_broadcast([128, 8, 8]), cgov, ALU.subtract,
                   [bcref, bcgo], [bbiaso])
            kvs = [(sbb(1024), sbb(1024)) for _ in range(3)]
            ptl = [sbb(512) for _ in range(2)]
            rd2, brd2 = sf(512); o2, bo2 = sf(512); rs2, brs2 = sf(512); osq2, bosq2 = sbb(512)
            OT = [0, 2]; DEN = [1, 3]; SB_ = [4, 5]; NSB = 6
            ki = [0]; pi = [0]; si = [0]
            for h in range(H):
                started = [False, False]
                for srcr in (0, 1, 2, 4):
                    (tk, btk), (tv, btv) = kvs[ki[0] % 3]; ki[0] += 1
                    if srcr < 4:
                        ksrc = rcvK[h // 4][srcr * 512 + (h % 4) * 128:srcr * 512 + (h % 4 + 1) * 128, :]
                        vsrc = [rcvV[i][srcr * 512:(srcr + 1) * 512, h * 128:(h + 1) * 128] for i in range(2)]
                        rb = [bRCV]
                    else:
                        ksrc = sndK[h // 4][(h % 4) * 128:(h % 4 + 1) * 128, :]
                        vsrc = [sndV[i][:, h * 128:(h + 1) * 128] for i in range(2)]
                        rb = [bSND]
                    dma("sp", tk, ksrc, rb, [btk])
                    tv3 = tv.rearrange("p (k d) -> p k d", d=128)
                    for i in range(2):
                        dma("sp", tv3[:, 4 * i:4 * i + 4, :], vsrc[i].rearrange("(k p) d -> p k d", p=128), rb, [btv])
                    for kb in range(8):
                        for qc in range(2):
                            if srcr == 4 and qc == 0 and kb >= 4:
                                continue
                            sbk = SB_[si[0] % 2]; si[0] += 1
                            mm(PSB[sbk][:, :], [(tk[:, kb * 128:(kb + 1) * 128], QT(h)[:, qc * 512:(qc + 1) * 512])],
                               [btk, bREG[h]], [bPS[sbk]])
                            tp, btp = ptl[pi[0] % 2]; pi[0] += 1
                            if srcr < 4:
                                bap = biasv[:, qc, srcr * 8 + kb, h:h + 1]; bb = bbias
                            else:
                                bap = biasov[:, qc, kb, h:h + 1]; bb = bbiaso
                            act(tp, PSB[sbk][:, :], AF.Exp, [bPS[sbk], bb], [btp], scale=SCALE, bias=bap)
                            if srcr == 4 and (kb // 4 == qc):
                                tt("pool", tp, tp, DMB[:, kb % 4, :], ALU.mult, [btp, bC], [btp])
                            last = (srcr == 4 and ((qc == 0 and kb == 3) or (qc == 1 and kb == 7)))
                            first = not started[qc]; started[qc] = True

                            def pv(e, tv=tv, tp=tp, kb=kb, qc=qc, first=first, last=last):
                                e.matmul(PSB[OT[qc]][:, :], lhsT=tv[:, kb * 128:(kb + 1) * 128], rhs=tp, start=first, stop=last)
                                return e.matmul(PSB[DEN[qc]][:, :], lhsT=ONEB[:], rhs=tp, start=first, stop=last)
                            P.op("pe", pv, reads=[btv, btp, bC], writes=[bPS[OT[qc]], bPS[DEN[qc]]])
                for qc in range(2):
                    P.op("dve", lambda e, qc=qc, rd2=rd2: e.reciprocal(out=rd2, in_=PSB[DEN[qc]][:, :]), reads=[bPS[DEN[qc]]], writes=[brd2])
                    tt("dve", o2, PSB[OT[qc]][:, :], rd2, ALU.mult, [bPS[OT[qc]], brd2], [bo2])
                    act(osq2, o2, AF.Square, [bo2], [bosq2])
                    mm(PSB[NSB][:, :], [(ONEB[:], osq2)], [bosq2, bC], [bPS[NSB]])
                    act(rs2, PSB[NSB][:, :], AF.Ln, [bPS[NSB]], [brs2], scale=1.0 / DH, bias=EPS)
                    act(rs2, rs2, AF.Exp, [brs2], [brs2], scale=-0.5)
                    stt("dve", XN[:, h, qc * 512:(qc + 1) * 512], o2, PAR[:, po + PL_GA + h:po + PL_GA + h + 1], rs2,
                        ALU.mult, ALU.mult, [bo2, brs2, bPAR], [bXN[h]])
            barrier()

            def ev_res(ti, n, a, z, ps, bps):
                tt("dve", X[:, ti, a:z], ps, X[:, ti, a:z], ALU.add, [bps, bX[ti]], [bX[ti]])
            if o_dbg is not None and l == 0 and "DBGX1" in DBG:
                dg32, bdg32 = sf(16 * NS)
                cp("dve", dg32.rearrange("p (k t) -> p k t", t=NS), XN[:, :, NP:NT], bXN, [bdg32])
                dma("sp", o_dbg, dg32, [bdg32], [])
            proj_fm(w_out, l * D, [128 * i for i in range(16)], KT, rhs_xn, bXN, ev_res)
            if o_dbg is not None and l == 0 and "DBGX1" in DBG:
                dg33, bdg33 = sf(16 * NS)
                cp("dve", dg33.rearrange("p (k t) -> p k t", t=NS), X[:, :, NP:NT], bX, [bdg33])
                dma("sp", o_dbg2, dg33, [bdg33], [])
            barrier()

            rmsnorm(po + PL_GMLP, xn_out)
            Hh = lambda t: REG[:, t, 0:NT]
            r32 = [sf(344), sf(344)]
            ri = [0]
            for g in range(4):
                def ev_up(ti, n, a, z, ps, bps):
                    t_, b_ = r32[ri[0] % 2]; ri[0] += 1
                    act(t_[:, 0:z - a], ps, AF.Relu, [bps], [b_])
                    tt("dve", Hh(ti)[:, a:z], t_[:, 0:z - a], t_[:, 0:z - a], ALU.mult, [b_], [bREG[ti]])
                proj_fm(w_up, l * D, [g * 2048 + 128 * i for i in range(16)], KT, rhs_xn, bXN, ev_up)
                proj_fm(w_down, l * DFF + g * 2048, [128 * i for i in range(16)], 16,
                        lambda kt, a, z: Hh(kt)[:, a:z], bREG, ev_res)
            barrier()

            rmsnorm(po + PL_GPLE, xn_out)
            bWP = bf("WP")
            gs = [sf(344), sf(344)]
            gi2 = [0]
            gate = {}

            def ev_gate(ti, n, a, z, ps, bps):
                t_, b_ = gs[gi2[0] % 2]; gi2[0] += 1
                act(t_[:, 0:z - a], ps, AF.Sigmoid, [bps], [b_])
                if ti % 2 == 0 and n == 0:
                    dma("pool", WP[:], w_ple[l * PLE:(l + 1) * PLE, ti * 128:ti * 128 + 256].rearrange("(k p) c -> p k c", p=128),
                        [], [bWP])
                k = nextps()
                mm(PSB[k][:, 0:z - a], [(WP[:, kk, (ti % 2) * 128:(ti % 2) * 128 + 128], PTE[:, kk, a:z]) for kk in range(2)],
                   [bWP, bPTE], [bPS[k]])
                tt("dve", t_[:, 0:z - a], PSB[k][:, 0:z - a], t_[:, 0:z - a], ALU.mult, [bPS[k], b_], [b_])
                tt("dve", X[:, ti, a:z], X[:, ti, a:z], t_[:, 0:z - a], ALU.add, [bX[ti], b_], [bX[ti]])
            proj_fm(w_gate, l * D, [128 * i for i in range(16)], KT, rhs_xn, bXN, ev_gate)
            if o_dbg is not None and l == 0 and "DBGX" in DBG:
                dg32, bdg32 = sf(16 * NS)
                cp("dve", dg32.rearrange("p (k t) -> p k t", t=NS), X[:, :, NP:NT], bX, [bdg32])
                dma("sp", o_dbg, dg32, [bdg32], [])
            barrier()

        yt = [sf(NT), sf(NT)]

        def y_out(kt, rstd, b_r, g):
            t_, b_ = yt[kt % 2]
            stt("dve", t_, X[:, kt, :], g, rstd, ALU.mult, ALU.mult, [bX[kt], b_r, bPAR], [b_])
            dma("sp", o_y[kt * 128:(kt + 1) * 128, :], t_, [b_], [])
        rmsnorm(P_GFIN, y_out)
    try:
        body()
    except _Stop:
        pass
    P.emit()
    es.close()
    return nc


def _finish(P, es, nc):
    P.emit()
    es.close()
    return nc


_NC = None


def kernel(**inp):
    global _NC
    inp = {k: np.asarray(v) for k, v in inp.items()}
    if _NC is None:
        _NC = build_program()
    nc = _NC
    f32 = np.float32
    cst = make_consts()
    shared = {
        "cst": cst,
        "w_in": np.ascontiguousarray(inp["w_in"].reshape(L * D, INW)),
        "w_out": np.ascontiguousarray(inp["w_out"].reshape(L * D, D)),
        "w_up": np.ascontiguousarray(inp["w_up"].reshape(L * D, DFF)),
        "w_down": np.ascontiguousarray(inp["w_down"].reshape(L * DFF, D)),
        "w_gate": np.ascontiguousarray(inp["w_ple_gate"].reshape(L * D, D)),
        "w_ple": np.ascontiguousarray(inp["w_ple"].reshape(L * PLE, D)),
    }
    if "TINYW" in DBG:
        for k in ("w_out", "w_up", "w_down", "w_gate", "w_ple"):
            shared[k] = np.zeros((128, 128), f32)
    for l in range(L):
        for hh in range(2):
            shared["ck%d_%d" % (l, hh)] = np.ascontiguousarray(
                inp["cache_k"][l][:, :, 4 * hh:4 * hh + 4].transpose(0, 2, 1, 3)).reshape(NPHYS * 64, 1024)
            shared["cv%d_%d" % (l, hh)] = np.ascontiguousarray(
                inp["cache_v"][l][:, :, 4 * hh:4 * hh + 4].transpose(0, 2, 1, 3)).reshape(NPHYS * 64, 1024)
        shared["clf%d" % l] = np.ascontiguousarray(inp["cache_logf"][l].transpose(0, 2, 1)).reshape(NPHYS * 8, 128)
    in_maps = []
    for c in range(8):
        b, j = c // 4, c % 4
        rows = slice(1024 * j, 1024 * (j + 1))
        m = dict(shared)
        m["xT"] = np.ascontiguousarray(np.concatenate([inp["x_prompt"][b, rows], inp["x_sample"][c]], 0).T.astype(f32))
        m["pT"] = np.ascontiguousarray(np.concatenate(
            [np.concatenate([inp["p_prompt"][l, b, rows], inp["p_sample"][l, c]], 0).T for l in range(L)], 0).astype(f32))
        m["scv"] = np.ascontiguousarray(np.concatenate([inp["state_conv"][l, c].T for l in range(L)], 0).astype(f32))
        m["par"] = pack_params(inp, j)
        m["ptab"] = np.ascontiguousarray(inp["page_table"][c].reshape(128, 1).astype(np.int32))
        in_maps.append(m)
    res = run_bass_kernel_spmd(nc, in_maps, core_ids=list(range(8))).results
    global _LAST
    _LAST = res

    y_prompt = np.empty((2, 4096, D), f32); y_sample = np.empty((8, 8, D), f32)
    k_prompt = np.empty((L, 2, 4096, H, DH), f32); v_prompt = np.empty((L, 2, 4096, H, DH), f32)
    lf_prompt = np.empty((L, 2, 4096, H), f32); cv_prompt = np.empty((L, 2, HALO, CW), f32)
    k_sample = np.empty((L, 8, 8, H, DH), f32); v_sample = np.empty((L, 8, 8, H, DH), f32)
    lf_sample = np.empty((L, 8, 8, H), f32); cv_sample = np.empty((L, 8, HALO, CW), f32)
    for c in range(8):
        b, j = c // 4, c % 4
        rows = slice(1024 * j, 1024 * (j + 1))
        r = res[c]
        yT = r["o_y"]
        y_prompt[b, rows] = yT[:, :NP].T
        y_sample[c] = yT[:, NP:].T
        ok = r["o_k"].reshape(L, 1024, NT); ov = r["o_v"].reshape(L, NT, 1024)
        olf = r["o_lf"].reshape(L, NT, 8); ocv = r["o_cv"].reshape(L, CW, 2 * HALO)
        for l in range(L):
            k_prompt[l, b, rows] = ok[l][:, :NP].T.reshape(1024, H, DH)
            k_sample[l, c] = ok[l][:, NP:].T.reshape(8, H, DH)
            v_prompt[l, b, rows] = ov[l][:NP].reshape(1024, H, DH)
            v_sample[l, c] = ov[l][NP:].reshape(8, H, DH)
            lf_prompt[l, b, rows] = olf[l][:NP]
            lf_sample[l, c] = olf[l][NP:]
            if j == 3:
                cv_prompt[l, b] = ocv[l][:, :HALO].T
            cv_sample[l, c] = ocv[l][:, HALO:].T
    return (y_prompt, y_sample, k_prompt, v_prompt, lf_prompt, cv_prompt, k_sample, v_sample, lf_sample, cv_sample)
```

```python
import numpy as np
from contextlib import ExitStack
import concourse.bass as bass
import concourse.mybir as mybir
from concourse.bass_utils import run_bass_kernel_spmd

F32 = mybir.dt.float32
BF16 = mybir.dt.bfloat16
I32 = mybir.dt.int32
AF = mybir.ActivationFunctionType
ALU = mybir.AluOpType
AX = mybir.AxisListType

D = 2048; KT = 16; NP = 1024; NS = 8; NT = NP + NS
L = 2; H = 8; DH = 128; CW = 1024; CK = 31; HALO = CK - 1
DFF = 8192; PLE = 256; NPHYS = 1280; NPG = 128
INW = 3 * 1024 + 8 + 2 * CW
EPS = 1e-6
SCALE = DH ** -0.5
CH = [(0, 344), (344, 688), (688, 1032)]
NEG = -30000.0
RS = 2048 + HALO + 16
ENGS = ("pe", "act", "dve", "pool", "sp")
DMA_POOL = 12


class Buf:
    __slots__ = ("name", "last_w", "readers", "excl")

    def __init__(self, name):
        self.name = name
        self.last_w = None
        self.readers = {}
        self.excl = False


class Op:
    __slots__ = ("eng", "fn", "deps", "signal", "sigval", "kind", "dsem", "dval")


class Prog:
    def __init__(self, nc):
        self.nc = nc
        self.ops = {e: [] for e in ENGS}
        self.dma_count = {q: [0] * DMA_POOL for q in ("pool", "sp")}
        self.dma_rr = {"pool": 0, "sp": 0}
        self.dma_last = {q: [None] * DMA_POOL for q in ("pool", "sp")}
        self.cc_count = 0
        self.cc_last = None
        self.phase = Buf("phase")

    def op(self, eng, fn, reads=(), writes=(), kind="c", nophase=False):
        o = Op()
        o.eng = eng; o.fn = fn; o.signal = False; o.sigval = None
        o.kind = kind; o.dsem = None; o.dval = None
        deps = []
        writes = list(writes) + [b for b in reads if b.excl and b not in writes]
        reads = [b for b in reads if not b.excl]
        if not nophase:
            reads.append(self.phase)
        for b in reads:
            if b.last_w is not None:
                deps.append(b.last_w)
        for b in writes:
            if b.last_w is not None:
                deps.append(b.last_w)
            deps.extend(b.readers.values())
        if kind == "dma":
            k = self.dma_rr[eng]
            self.dma_rr[eng] = (k + 1) % DMA_POOL
            self.dma_count[eng][k] += 1
            o.dsem = (eng, k)
            o.dval = 16 * self.dma_count[eng][k]
            if self.dma_last[eng][k] is not None:
                deps.append(self.dma_last[eng][k])
            self.dma_last[eng][k] = o
        elif kind == "cc":
            self.cc_count += 1
            o.dsem = ("cc", 0)
            o.dval = self.cc_count
            if self.cc_last is not None:
                deps.append(self.cc_last)
            self.cc_last = o
        fl = []
        for d in deps:
            if d is o:
                continue
            if d.kind == "c" and d.eng == eng and eng == "pe":
                continue
            fl.append(d)
        o.deps = fl
        for d in fl:
            if d.kind == "c":
                d.signal = True
        for b in reads:
            key = eng if kind == "c" else ("x", id(o))
            b.readers[key] = o
        for b in writes:
            b.last_w = o
            b.readers = {}
        self.ops[eng].append(o)
        return o

    def barrier(self, fn):
        self.op("dve", fn, writes=[self.phase], nophase=True)

    def emit(self):
        nc = self.nc
        with ExitStack() as es:
            csem = {e: es.enter_context(nc.semaphore("c_" + e)) for e in ("pe", "act", "dve", "pool")}
            dsem = {}
            for q in ("pool", "sp"):
                for k in range(DMA_POOL):
                    if self.dma_count[q][k] > 0:
                        dsem[(q, k)] = es.enter_context(nc.semaphore("d_%s%d" % (q, k)))
            dsem[("cc", 0)] = es.enter_context(nc.semaphore("ccs"))
            for e in ENGS:
                c = 0
                for o in self.ops[e]:
                    if o.kind == "c" and o.signal:
                        c += 1
                        o.sigval = c
            block = es.enter_context(nc.Block())
            prog = self

            def run(ename, eobj):
                waited = {}
                for o in prog.ops[ename]:
                    need = {}
                    for d in o.deps:
                        if d.kind == "c":
                            key = ("c", d.eng); val = d.sigval
                        else:
                            key = ("d",) + d.dsem; val = d.dval
                        if need.get(key, 0) < val:
                            need[key] = val
                    for key, val in need.items():
                        if waited.get(key, 0) >= val:
                            continue
                        waited[key] = val
                        sem = dsem[key[1:]] if key[0] == "d" else csem[key[1]]
                        eobj.wait_ge(sem, val)
                    ins = o.fn(eobj)
                    if o.kind == "dma":
                        ins.then_inc(dsem[o.dsem], 16)
                    elif o.kind == "cc":
                        ins.then_inc(dsem[o.dsem])
                    elif o.signal:
                        ins.then_inc(csem[ename], 1)
                if ename in ("pool", "sp"):
                    for k in range(DMA_POOL):
                        if prog.dma_count[ename][k] > 0:
                            eobj.wait_ge(dsem[(ename, k)], 16 * prog.dma_count[ename][k])
                    if ename == "pool" and prog.cc_count:
                        eobj.wait_ge(dsem[("cc", 0)], prog.cc_count)

            @block.tensor
            def _(e):
                run("pe", e)

            @block.scalar
            def _(e):
                run("act", e)

            @block.vector
            def _(e):
                run("dve", e)

            @block.gpsimd
            def _(e):
                run("pool", e)

            @block.sync
            def _(e):
                run("sp", e)


C_ID = 0; C_TRI = 128; C_SUP = 256; C_UPS = 384; C_DM = 512; C_CAUS = 2560; C_OFF64 = 2568
C_OFF8 = 2632; C_ONE = 2640; NCST = 2768


def make_consts():
    c = np.zeros((128, NCST), np.float32)
    i = np.arange(128)
    c[:, C_ID:C_ID + 128] = np.eye(128)
    c[:, C_TRI:C_TRI + 128] = (i[:, None] <= i[None, :])
    c[:, C_SUP:C_SUP + 128] = (i[:, None] > i[None, :])
    c[:, C_UPS:C_UPS + 128] = (i[:, None] > i[None, :])
    f = np.arange(512)
    for k in range(4):
        c[:, C_DM + 512 * k:C_DM + 512 * (k + 1)] = (128 * k + i[:, None] <= f[None, :])
    c[:8, C_CAUS:C_CAUS + 8] = (np.arange(8)[:, None] <= np.arange(8)[None, :])
    c[:, C_OFF64:C_OFF64 + 64] = np.arange(64)[None, :]
    c[:, C_OFF8:C_OFF8 + 8] = np.arange(8)[None, :]
    c[:, C_ONE:C_ONE + 128] = 1.0
    return c


PL_GMIX = 0; PL_GMLP = 16; PL_GPLE = 32; PL_CW = 48; PL_CB = 296; PL_LG = 304; PL_LB = 312
PL_GA = 320; PL_BF = 328; PL_N = 336
P_GFIN = 2 * PL_N; P_PREV = P_GFIN + 16; P_OWN = P_PREV + 4; P_MB = P_OWN + 4; P_WQ = P_MB + 32
NPAR = P_WQ + 64


def pack_params(inp, j):
    p = np.zeros((128, NPAR), np.float32)
    fm = lambda v: np.ascontiguousarray(v.reshape(-1, 128).T)
    for l in range(L):
        o = l * PL_N
        p[:, o + PL_GMIX:o + PL_GMIX + 16] = fm(inp["g_mix"][l])
        p[:, o + PL_GMLP:o + PL_GMLP + 16] = fm(inp["g_mlp"][l])
        p[:, o + PL_GPLE:o + PL_GPLE + 16] = fm(inp["g_ple"][l])
        cw = inp["conv_w"][l]
        p[:, o + PL_CW:o + PL_CW + 248] = cw.reshape(CK, 8, 128).transpose(2, 1, 0).reshape(128, 248)
        p[:, o + PL_CB:o + PL_CB + 8] = fm(inp["conv_b"][l])
        p[:, o + PL_LG:o + PL_LG + 8] = fm(inp["conv_ln_g"][l])
        p[:, o + PL_LB:o + PL_LB + 8] = fm(inp["conv_ln_b"][l])
        p[:, o + PL_GA:o + PL_GA + 8] = fm(inp["g_attn_out"][l])
        p[:, o + PL_BF:o + PL_BF + 8] = inp["b_f"][l][None, :]
    p[:, P_GFIN:P_GFIN + 16] = fm(inp["g_final"])
    for r in range(4):
        p[:, P_PREV + r] = 1.0 if r == j - 1 else 0.0
        p[:, P_OWN + r] = 1.0 if r == j else 0.0
        p[:, P_MB + 8 * r:P_MB + 8 * r + 8] = 0.0 if r < j else NEG
    for qc in range(2):
        nblk = 8 * j + 4 * qc
        p[:, P_WQ + 32 * qc:P_WQ + 32 * qc + nblk] = 1.0
    return p


class _Stop(Exception):
    pass


KSTOP = None
import os as _os
DBG = set(_os.environ.get('KDBG', '').split(','))


def build_program():
    try:
        return _build_program()
    finally:
        pass


def _build_program():
    nc = bass.Bass("TRN2", target_bir_lowering=False, num_devices=8)
    dt_in = lambda n, s, d=F32: nc.dram_tensor(n, s, d, kind="ExternalInput").ap()
    dt_out = lambda n, s, d=F32: nc.dram_tensor(n, s, d, kind="ExternalOutput").ap()
    xT = dt_in("xT", [D, NT])
    pT = dt_in("pT", [L * PLE, NT])
    scv = dt_in("scv", [L * CW, HALO])
    par = dt_in("par", [128, NPAR])
    cst = dt_in("cst", [128, NCST])
    ptab = dt_in("ptab", [128, 1], I32)
    w_in = dt_in("w_in", [L * D, INW])
    if "TINYW" in DBG:
        w_out = dt_in("w_out", [128, 128]); w_up = dt_in("w_up", [128, 128]); w_down = dt_in("w_down", [128, 128])
        w_gate = dt_in("w_gate", [128, 128]); w_ple = dt_in("w_ple", [128, 128])
    else:
        w_out = dt_in("w_out", [L * D, D])
        w_up = dt_in("w_up", [L * D, DFF])
        w_down = dt_in("w_down", [L * DFF, D])
        w_gate = dt_in("w_gate", [L * D, D])
        w_ple = dt_in("w_ple", [L * PLE, D])
    ck = [[dt_in("ck%d_%d" % (l, hh), [NPHYS * 64, 1024]) for hh in range(2)] for l in range(L)]
    cv = [[dt_in("cv%d_%d" % (l, hh), [NPHYS * 64, 1024]) for hh in range(2)] for l in range(L)]
    clf = [dt_in("clf%d" % l, [NPHYS * 8, 128]) for l in range(L)]
    o_y = dt_out("o_y", [D, NT])
    o_k = dt_out("o_k", [L * 1024, NT])
    o_v = dt_out("o_v", [L * NT, 1024])
    o_lf = dt_out("o_lf", [L * NT, 8])
    o_cv = dt_out("o_cv", [L * CW, 2 * HALO])
    o_dbg = dt_out("o_dbg", [128, 16 * NS]) if "DBGOUT" in DBG else None
    o_dbg2 = dt_out("o_dbg2", [128, 16 * NS]) if "DBGOUT" in DBG else None
    sndK = [nc.dram_tensor("sndK%d" % i, [512, 1024], BF16, kind="Internal").ap() for i in range(2)]
    rcvK = [nc.dram_tensor("rcvK%d" % i, [4 * 512, 1024], BF16, kind="Internal").ap() for i in range(2)]
    sndV = [nc.dram_tensor("sndV%d" % i, [512, 1024], BF16, kind="Internal").ap() for i in range(2)]
    rcvV = [nc.dram_tensor("rcvV%d" % i, [4 * 512, 1024], BF16, kind="Internal").ap() for i in range(2)]
    sndM = nc.dram_tensor("sndM", [64, 1024], BF16, kind="Internal").ap()
    rcvM = nc.dram_tensor("rcvM", [4 * 64, 1024], BF16, kind="Internal").ap()
    LFR = 2048 + HALO
    sndf = sndM[32:48, :].bitcast(F32).rearrange("r (q h) -> (r q) h", h=8)
    rcvf = [rcvM[r * 64 + 32:r * 64 + 48, :].bitcast(F32).rearrange("r (q h) -> (r q) h", h=8) for r in range(4)]
    RG = [[0, 1, 2, 3], [4, 5, 6, 7]]

    P = Prog(nc)
    es = ExitStack()
    sb = lambda n, s, d: es.enter_context(nc.sbuf_tensor(n, s, d))
    X = sb("X", [128, KT, NT], F32)
    XN = sb("XN", [128, KT, NT], BF16)
    REG = sb("REG", [128, 16, NT + HALO], BF16)
    PAR = sb("PAR", [128, NPAR], F32)
    CF = sb("CF", [128, 512], F32)
    ONE32 = sb("ONE32", [128, 128], F32)
    IDB = sb("IDB", [128, 128], BF16)
    ONEB = sb("ONEB", [128, 128], BF16)
    DMB = sb("DMB", [128, 4, 512], BF16)
    CAUS = sb("CAUS", [128, 8], F32)
    OFF = sb("OFF", [128, 72], F32)
    WS = [sb("WS%d" % i, [128, 16, 256], BF16) for i in range(2)]
    WF = sb("WF", [128, 16, 8], BF16)
    WP = sb("WP", [128, 2, 256], BF16)
    PTE = sb("PTE", [128, 2, NT], BF16)
    KTS = sb("KTS", [128, H, NS], BF16)
    VTS = sb("VTS", [128, 1024], BF16)
    LF = sb("LF", [128, 9, 8], F32)
    US32 = sb("US32", [128, 8, HALO + NS], F32)
    USB = sb("USB", [128, 8, HALO + NS], BF16)
    CVO = sb("CVO", [128, 8, HALO], F32)
    SCF = sb("SCF", [128, 6400], F32)
    SCB = sb("SCB", [128, 7808], BF16)
    SCI = sb("SCI", [128, 144], I32)
    PSB = [es.enter_context(nc.psum_tensor("ps%d" % i, [128, 512], F32)) for i in range(7)]
    PST = es.enter_context(nc.psum_tensor("pst", [128, 1024], BF16))

    B = {}

    def bf(name):
        if name not in B:
            B[name] = Buf(name)
        return B[name]
    bX = [bf("X%d" % k) for k in range(KT)]
    bXN = [bf("XN%d" % k) for k in range(KT)]
    bREG = [bf("REG%d" % k) for k in range(16)]
    bPS = [bf("PS%d" % k) for k in range(7)]
    bPST = bf("PST")
    for b_ in bPS + [bPST]:
        b_.excl = True
    bWS = [bf("WS0"), bf("WS1")]
    wsi = [0]
    psi = [0]

    scr = {"f": 0, "b": 0, "n": 0}

    def sf(n):
        o = scr["f"]; scr["f"] += n
        assert scr["f"] <= 6400, scr
        return SCF[:, o:o + n], bf("sf@%d" % o)

    def sbb(n):
        o = scr["b"]; scr["b"] += n
        assert scr["b"] <= 7808, scr
        return SCB[:, o:o + n], bf("sb@%d" % o)

    stage = [0]

    def ckpt(name):
        if _os.environ.get("KCUT") == name:
            raise _Stop()

    def barrier():
        if o_dbg is not None and _os.environ.get("KDUMP") == str(stage[0] + 1):
            d1, bd1 = sf(16 * NS); d2, bd2 = sf(16 * NS)
            cp("dve", d1.rearrange("p (k t) -> p k t", t=NS), XN[:, :, NP:NT], bXN, [bd1])
            dma("sp", o_dbg, d1, [bd1], [])
            cp("dve", d2.rearrange("p (k t) -> p k t", t=NS), X[:, :, NP:NT], bX, [bd2])
            dma("sp", o_dbg2, d2, [bd2], [])
        P.barrier(lambda e: e.memset(SCI[:, 2:3], 0))
        scr["f"] = 0; scr["b"] = 0
        stage[0] += 1
        if KSTOP is not None and stage[0] >= KSTOP:
            raise _Stop()

    def dma(q, out, in_, reads, writes, nophase=False, **kw):
        P.op(q, lambda e: e.dma_start(out=out, in_=in_, **kw), reads=reads, writes=writes, kind="dma", nophase=nophase)

    def act(out, in_, func, reads, writes, **kw):
        P.op("act", lambda e: e.activation(out=out, in_=in_, func=func, **kw), reads=reads, writes=writes)

    def tt(eng, out, in0, in1, op, reads, writes):
        P.op(eng, lambda e: e.tensor_tensor(out=out, in0=in0, in1=in1, op=op), reads=reads, writes=writes)

    def ts(eng, out, in0, s1, s2, op0, op1, reads, writes):
        if op1 is None:
            P.op(eng, lambda e: e.tensor_scalar(out=out, in0=in0, scalar1=s1, scalar2=None, op0=op0),
                 reads=reads, writes=writes)
        else:
            P.op(eng, lambda e: e.tensor_scalar(out=out, in0=in0, scalar1=s1, scalar2=s2, op0=op0, op1=op1),
                 reads=reads, writes=writes)

    def stt(eng, out, in0, scalar, in1, op0, op1, reads, writes):
        P.op(eng, lambda e: e.scalar_tensor_tensor(out=out, in0=in0, scalar=scalar, in1=in1, op0=op0, op1=op1),
             reads=reads, writes=writes)

    def cp(eng, out, in_, reads, writes):
        P.op(eng, lambda e: e.tensor_copy(out=out, in_=in_), reads=reads, writes=writes)

    def mm(out, pairs, reads, writes, start=True, stop=True):
        def fn(e):
            ins = None
            n = len(pairs)
            for i, (l_, r_) in enumerate(pairs):
                ins = e.matmul(out, lhsT=l_, rhs=r_, start=(start and i == 0), stop=(stop and i == n - 1))
            return ins
        P.op("pe", fn, reads=reads, writes=writes)

    def nextps():
        k = psi[0]; psi[0] = (k + 1) % 4
        return k

    def body():
        bPAR = bf("PAR"); bC = bf("CONST")
        dma("sp", PAR[:], par, [], [bPAR])
        t_c, b_c = sf(NCST)
        dma("sp", t_c, cst, [], [b_c])
        cp("dve", CF[:], t_c[:, 0:512], [b_c], [bC])
        cp("dve", ONE32[:], t_c[:, C_ONE:C_ONE + 128], [b_c], [bC])
        cp("dve", IDB[:], t_c[:, C_ID:C_ID + 128], [b_c], [bC])
        cp("dve", ONEB[:], t_c[:, C_ONE:C_ONE + 128], [b_c], [bC])
        cp("dve", DMB[:].rearrange("p a f -> p (a f)"), t_c[:, C_DM:C_DM + 2048], [b_c], [bC])
        cp("dve", CAUS[:], t_c[:, C_CAUS:C_CAUS + 8], [b_c], [bC])
        cp("dve", OFF[:], t_c[:, C_OFF64:C_OFF64 + 72], [b_c], [bC])
        for kt in range(KT):
            dma("sp", X[:, kt, :], xT[kt * 128:(kt + 1) * 128, :], [], [bX[kt]])
        bIDX = bf("IDX")
        dma("sp", SCI[:, 0:1], ptab, [], [bIDX])
        t_pf, b_pf = sf(144)
        offs, b_offs = sf(136)
        cp("dve", t_pf[:, 0:1], SCI[:, 0:1], [bIDX], [b_pf])
        cp("dve", offs[:, 0:64], OFF[:, 0:64], [bC], [b_offs])
        ts("dve", offs[:, 64:128], OFF[:, 0:64], 64.0, None, ALU.add, None, [bC], [b_offs])
        cp("dve", offs[:, 128:136], OFF[:, 64:72], [bC], [b_offs])
        ts("dve", t_pf[:, 1:2], t_pf[:, 0:1], 64.0, None, ALU.mult, None, [b_pf], [b_pf])
        ts("dve", t_pf[:, 8:72], offs[:, 0:64], t_pf[:, 1:2], None, ALU.add, None, [b_offs, b_pf], [b_pf])
        ts("dve", t_pf[:, 72:136], offs[:, 0:64], t_pf[:, 1:2], None, ALU.add, None, [b_offs, b_pf], [b_pf])
        ts("dve", t_pf[:, 2:3], t_pf[:, 0:1], 8.0, None, ALU.mult, None, [b_pf], [b_pf])
        ts("dve", t_pf[:, 136:144], offs[:, 128:136], t_pf[:, 2:3], None, ALU.add, None, [b_offs, b_pf], [b_pf])
        cp("dve", SCI[:, 8:144], t_pf[:, 8:144], [b_pf], [bIDX])
        IDXK = lambda h, sq: SCI[:, 8 + h * 16 + sq:8 + h * 16 + sq + 1]
        IDXL = lambda h: SCI[:, 136 + h:137 + h]
        barrier()

        def rmsnorm(gcol, out_fn):
            sq = [sbb(NT), sbb(NT)]
            pss = [4, 5, 6]
            for kt in range(KT):
                t_, b_ = sq[kt % 2]
                act(t_, X[:, kt, :], AF.Square, [bX[kt]], [b_])
                for n, (a, z) in enumerate(CH):
                    mm(PSB[pss[n]][:, 0:z - a], [(ONEB[:], t_[:, a:z])], [b_, bC], [bPS[pss[n]]],
                       start=(kt == 0), stop=(kt == KT - 1))
            rstd, b_r = sf(NT)
            for n, (a, z) in enumerate(CH):
                act(rstd[:, a:z], PSB[pss[n]][:, 0:z - a], AF.Ln, [bPS[pss[n]]], [b_r], scale=1.0 / D, bias=EPS)
            act(rstd, rstd, AF.Exp, [b_r], [b_r], scale=-0.5)
            for kt in range(KT):
                out_fn(kt, rstd, b_r, PAR[:, gcol + kt:gcol + kt + 1])

        def xn_out(kt, rstd, b_r, g):
            stt("dve", XN[:, kt, :], X[:, kt, :], g, rstd, ALU.mult, ALU.mult, [bX[kt], b_r, bPAR], [bXN[kt]])

        def load_w(slot, wap, row0, nkt, c0, ncols, dst_c0=0):
            src = wap[row0:row0 + nkt * 128, c0:c0 + ncols].rearrange("(k p) c -> p k c", p=128)
            dma("pool", WS[slot][:, 0:nkt, dst_c0:dst_c0 + ncols], src, [], [bWS[slot]], nophase=True)

        def load_w256(slot, wap, row0, nkt, c0):
            hk = nkt // 2
            for k0 in (0, hk):
                src = wap[row0 + k0 * 128:row0 + (k0 + hk) * 128, c0:c0 + 256].rearrange("(k p) c -> p k c", p=128)
                dma("pool", WS[slot][:, k0:k0 + hk, 0:256], src, [], [bWS[slot]], nophase=True)

        def proj_fm(wap, row0, col_tiles, nkt, rhs, rhs_bufs, evac):
            for t0 in range(0, len(col_tiles), 2):
                grp = col_tiles[t0:t0 + 2]
                slot = wsi[0]; wsi[0] ^= 1
                if len(grp) == 2 and grp[1] == grp[0] + 128:
                    load_w256(slot, wap, row0, nkt, grp[0])
                else:
                    for gi, c0 in enumerate(grp):
                        load_w(slot, wap, row0, nkt, c0, 128, dst_c0=128 * gi)
                for gi, c0 in enumerate(grp):
                    for n, (a, z) in enumerate(CH):
                        k = nextps()
                        mm(PSB[k][:, 0:z - a],
                           [(WS[slot][:, kt, 128 * gi:128 * gi + 128], rhs(kt, a, z)) for kt in range(nkt)],
                           [bWS[slot]] + rhs_bufs, [bPS[k]])
                        evac(t0 + gi, n, a, z, PSB[k][:, 0:z - a], bPS[k])

        for l in range(L):
            po = l * PL_N
            bPTE = bf("PTE")
            pTl = pT[l * PLE:(l + 1) * PLE, :].rearrange("(k p) n -> p k n", p=128)
            for c0, c1 in ((0, 512), (512, NP), (NP, NT)):
                dma("pool", PTE[:, :, c0:c1], pTl[:, :, c0:c1], [], [bPTE])
            bUS = bf("US")
            dma("sp", US32[:, :, 0:HALO], scv[l * CW:(l + 1) * CW, :].rearrange("(c p) t -> p c t", p=128), [], [bUS])
            rmsnorm(po + PL_GMIX, xn_out)
            ckpt("A1")
            rhs_xn = lambda kt, a, z: XN[:, kt, a:z]
            QT = lambda h: REG[:, h, 0:NT]
            U = lambda c: REG[:, 8 + c, :]
            bCVO = bf("CVO")
            glu = {}

            def ev_conv(ti, n, a, z, ps, bps):
                c, isg = ti // 2, ti % 2
                if not isg:
                    v32, bv = sf(344)
                    cp("dve", v32[:, 0:z - a], ps, [bps], [bv])
                    glu[(c, n)] = (v32, bv)
                else:
                    v32, bv = glu[(c, n)]
                    sg, bs = sf(344)
                    act(sg[:, 0:z - a], ps, AF.Sigmoid, [bps], [bs])
                    tt("dve", v32[:, 0:z - a], v32[:, 0:z - a], sg[:, 0:z - a], ALU.mult, [bv, bs], [bv])
                    cp("dve", U(c)[:, HALO + a:HALO + z], v32[:, 0:z - a], [bv], [bREG[8 + c]])
                    if n == 2:
                        cp("dve", CVO[:, c, :], v32[:, 994 - a:1024 - a], [bv], [bCVO])
                        cp("dve", US32[:, c, HALO:HALO + NS], v32[:, 1024 - a:1032 - a], [bv], [bUS])
                    if (c * 3 + n) % 4 == 3:
                        scr["f"] -= 0
            base_f = scr["f"]
            ctiles = []
            for c in range(8):
                ctiles += [3080 + 128 * c, 3080 + 1024 + 128 * c]
            for c in range(8):
                scr["f"] = base_f + (c % 2) * 6 * 344
                proj_fm(w_in, l * D, ctiles[2 * c:2 * c + 2], KT, rhs_xn, bXN,
                        lambda ti, n, a, z, ps, bps, c=c: ev_conv(2 * c + ti, n, a, z, ps, bps))
            scr["f"] = base_f + 12 * 344
            ckpt("A2")
            cp("dve", USB[:], US32[:], [bUS], [bUS])
            dma("sp", o_cv[l * CW:(l + 1) * CW, 0:HALO].rearrange("(c p) t -> p c t", p=128), CVO[:], [bCVO], [])
            dma("sp", o_cv[l * CW:(l + 1) * CW, HALO:2 * HALO].rearrange("(c p) t -> p c t", p=128),
                US32[:, :, NS:NS + HALO], [bUS], [])
            bSND = bf("SND")
            for c in range(8):
                dst = sndM[0:HALO, :].rearrange("r c -> (r c)").rearrange("(c p t) -> p c t", p=128, t=HALO)
                dma("sp", dst[:, c, :], U(c)[:, HALO + 994:HALO + 1024], [bREG[8 + c]], [bSND])

            ckpt("A3")
            bKTS = bf("KTS")
            kb16 = [sbb(NT), sbb(NT)]
            k32 = [sf(344), sf(344)]
            cnt = [0]

            def ev_qk(ti, n, a, z, ps, bps):
                if ti < 8:
                    act(QT(ti)[:, a:z], ps, AF.Identity, [bps], [bREG[ti]])
                else:
                    h = ti - 8
                    t16, b16 = kb16[h % 2]
                    act(t16[:, a:z], ps, AF.Identity, [bps], [b16])
                    t32, b32 = k32[cnt[0] % 2]; cnt[0] += 1
                    cp("dve", t32[:, 0:z - a], ps, [bps], [b32])
                    if "NOKOUT" not in DBG:
                        dma("sp", o_k[l * 1024 + h * 128:l * 1024 + (h + 1) * 128, a:z], t32[:, 0:z - a], [b32], [])
                    if n == 2:
                        if "NOSNDK" not in DBG:
                            dma("sp", sndK[h // 4][(h % 4) * 128:(h % 4 + 1) * 128, :], t16[:, 0:NP], [b16], [bSND])
                        if "NOKTS" not in DBG:
                            cp("dve", KTS[:, h, :], t16[:, NP:NT], [b16], [bKTS])
            proj_fm(w_in, l * D, [128 * i for i in range(8 if "QONLY" in DBG else 16)], KT, rhs_xn, bXN, ev_qk)

            ckpt("A4")
            bWF = bf("WF"); bLF = bf("LF"); bVTS = bf("VTS")
            dma("pool", WF[:], w_in[l * D:(l + 1) * D, 3072:3080].rearrange("(k p) c -> p k c", p=128), [], [bWF])
            v32 = [sf(256), sf(256)]
            v16 = [sbb(256), sbb(256)]
            cnt2 = [0]
            for vs in range(4):
                slot = wsi[0]; wsi[0] ^= 1
                load_w256(slot, w_in, l * D, KT, 2048 + 256 * vs)
                for tb in range(9):
                    m = 128 if tb < 8 else NS
                    k = nextps()
                    mm(PSB[k][0:m, 0:256], [(XN[:, kt, tb * 128:tb * 128 + m], WS[slot][:, kt, :]) for kt in range(KT)],
                       [bWS[slot]] + bXN, [bPS[k]])
                    t32, b32 = v32[cnt2[0] % 2]; t16, b16 = v16[cnt2[0] % 2]; cnt2[0] += 1
                    cp("dve", t32[0:m, :], PSB[k][0:m, 0:256], [bPS[k]], [b32])
                    dma("sp", o_v[l * NT + tb * 128:l * NT + tb * 128 + m, 256 * vs:256 * (vs + 1)], t32[0:m, :], [b32], [])
                    if tb < 8:
                        act(t16[0:m, :], PSB[k][0:m, 0:256], AF.Identity, [bPS[k]], [b16])
                        dma("sp", sndV[tb // 4][(tb % 4) * 128:(tb % 4 + 1) * 128, 256 * vs:256 * (vs + 1)], t16[:, :], [b16], [bSND])
                    else:
                        act(VTS[0:m, 256 * vs:256 * (vs + 1)], PSB[k][0:m, 0:256], AF.Identity, [bPS[k]], [bVTS])
            for tb in range(9):
                m = 128 if tb < 8 else NS
                k = nextps()
                mm(PSB[k][0:m, 0:8], [(XN[:, kt, tb * 128:tb * 128 + m], WF[:, kt, :]) for kt in range(KT)],
                   [bWF] + bXN, [bPS[k]])
                tt("dve", LF[0:m, tb, :], PSB[k][0:m, 0:8], PAR[0:m, po + PL_BF:po + PL_BF + 8], ALU.add, [bPS[k], bPAR], [bLF])
            ckpt("A5")
            act(LF[:, 0:8, :], LF[:, 0:8, :], AF.Exp, [bLF], [bLF], scale=-1.0)
            act(LF[0:NS, 8, :], LF[0:NS, 8, :], AF.Exp, [bLF], [bLF], scale=-1.0)
            act(LF[:, 0:8, :], LF[:, 0:8, :], AF.Ln, [bLF], [bLF], bias=1.0)
            act(LF[0:NS, 8, :], LF[0:NS, 8, :], AF.Ln, [bLF], [bLF], bias=1.0)
            ts("dve", LF[:, 0:8, :], LF[:, 0:8, :], -1.0, None, ALU.mult, None, [bLF], [bLF])
            ts("dve", LF[0:NS, 8, :], LF[0:NS, 8, :], -1.0, None, ALU.mult, None, [bLF], [bLF])
            bSNDF = bf("SNDF")
            dma("sp", o_lf[l * NT:l * NT + NP, :].rearrange("(k p) h -> p k h", p=128), LF[:, 0:8, :], [bLF], [])
            dma("sp", o_lf[l * NT + NP:(l + 1) * NT, :], LF[0:NS, 8, :], [bLF], [])
            dma("sp", sndf.rearrange("(k p) h -> p k h", p=128), LF[:, 0:8, :], [bLF], [bSND])
            ckpt("A6")
            bRCV = bf("RCV"); bRCVF = bf("RCVF")
            if "NOCC" not in DBG:
                for s_, r_ in [(sndM, rcvM), (sndK[0], rcvK[0]), (sndK[1], rcvK[1]), (sndV[0], rcvV[0]), (sndV[1], rcvV[1])]:
                    P.op("pool", lambda e, s_=s_, r_=r_: e.collective_compute("AllGather", ALU.bypass, replica_groups=RG,
                         ins=[s_], outs=[r_]), reads=[bSND], writes=[bRCV], kind="cc")
            barrier()

            kg = [sbb(1024), sbb(1024), sbb(1024)]
            vg = [sbb(1024), sbb(1024), sbb(1024)]
            ktt, bktt = sbb(1024)
            pts, bpts = sbb(64)
            lg, blg = sf(128); lgt, blgt = sf(128); tb_, btb = sf(128); er, ber = sf(128)
            pex, bpex = sf(128)
            ncn, bncn = sf(8); ptn, bptn = sbb(8)
            tsum, btsum = sf(1)
            OTS, DENS, SP, RP = 0, 1, 2, 3
            mm(PSB[RP][0:NS, 0:8], [(CF[0:NS, C_TRI:C_TRI + NS], LF[0:NS, 8, :])], [bC, bLF], [bPS[RP]])
            ts("dve", ncn[0:NS, :], PSB[RP][0:NS, 0:8], -1.0, None, ALU.mult, None, [bPS[RP]], [bncn])
            gi = [0]
            pend = [None]
            for h in range(H):
                P.op("pool", lambda e, h=h, l=l, lg=lg: e.indirect_dma_start(out=lg, out_offset=None, in_=clf[l],
                     in_offset=bass.IndirectOffsetOnAxis(ap=IDXL(h), axis=0)), reads=[bIDX], writes=[blg], kind="dma")
                P.op("pe", lambda e, lg=lg: e.transpose(PSB[RP][:, 0:128], lg, CF[:, C_ID:C_ID + 128]), reads=[blg, bC], writes=[bPS[RP]])
                cp("dve", lgt, PSB[RP][:, 0:128], [bPS[RP]], [blgt])
                P.op("dve", lambda e, tsum=tsum, lg=lg: e.tensor_reduce(out=tsum, in_=lg, axis=AX.X, op=ALU.add), reads=[blg], writes=[btsum])
                ts("dve", tb_, ONE32[:], tsum[:, 0:1], None, ALU.mult, None, [bC, btsum], [btb])
                mm(PSB[RP][:, 0:128], [(lgt, CF[:, C_UPS:C_UPS + 128]), (CF[:, C_SUP:C_SUP + 128], tb_)],
                   [blgt, btb, bC], [bPS[RP]])
                act(er, PSB[RP][:, 0:128], AF.Exp, [bPS[RP]], [ber])
                for sq in range(16):
                    tk, btk = kg[gi[0] % 3]; tv, btv = vg[gi[0] % 3]; gi[0] += 1
                    P.op("pool", lambda e, tk=tk, h=h, sq=sq, l=l: e.indirect_dma_start(out=tk, out_offset=None, in_=ck[l][h // 4],
                         in_offset=bass.IndirectOffsetOnAxis(ap=IDXK(h, sq), axis=0)), reads=[bIDX], writes=[btk], kind="dma")
                    P.op("pool", lambda e, tv=tv, h=h, sq=sq, l=l: e.indirect_dma_start(out=tv, out_offset=None, in_=cv[l][h // 4],
                         in_offset=bass.IndirectOffsetOnAxis(ap=IDXK(h, sq), axis=0)), reads=[bIDX], writes=[btv], kind="dma")

                    def tfn(e, tk=tk):
                        ins = None
                        for s in range(8):
                            ins = e.transpose(PST[:, s * 128:(s + 1) * 128], tk[:, s * 128:(s + 1) * 128], IDB[:])
                        return ins
                    P.op("pe", tfn, reads=[btk, bC], writes=[bPST])
                    act(ktt, PST[:, :], AF.Identity, [bPST], [bktt])
                    if pend[0] is not None:
                        pend[0](); pend[0] = None

                    def sfn(e, h=h, ktt=ktt, QT=QT):
                        ins = None
                        for s in range(8):
                            ins = e.matmul(PSB[SP][:, s * 8:(s + 1) * 8], lhsT=ktt[:, s * 128:(s + 1) * 128],
                                           rhs=QT(h)[:, NP:NT], start=True, stop=True)
                        return ins
                    P.op("pe", sfn, reads=[bktt, bREG[h]], writes=[bPS[SP]])
                    act(pex[:, 0:64], PSB[SP][:, 0:64], AF.Exp, [bPS[SP]], [bpex], scale=SCALE)
                    tt("dve", pts.rearrange("p (s t) -> p s t", t=NS), pex[:, 0:64].rearrange("p (s t) -> p s t", t=NS),
                       er[:, sq * 8:(sq + 1) * 8].unsqueeze(2).to_broadcast([128, 8, NS]), ALU.mult, [bpex, ber], [bpts])

                    def pvfn(e, h=h, sq=sq, tv=tv, pts=pts):
                        ins = None
                        for s in range(8):
                            st = (sq == 0 and s == 0)
                            e.matmul(PSB[OTS][:, h * 8:(h + 1) * 8], lhsT=tv[:, s * 128:(s + 1) * 128],
                                     rhs=pts[:, s * 8:(s + 1) * 8], start=st, stop=False)
                            ins = e.matmul(PSB[DENS][:, h * 8:(h + 1) * 8], lhsT=ONEB[:],
                                           rhs=pts[:, s * 8:(s + 1) * 8], start=st, stop=False)
                        return ins
                    pend[0] = lambda pvfn=pvfn, btv=btv: P.op("pe", pvfn, reads=[btv, bpts, bC], writes=[bPS[OTS], bPS[DENS]])
                pend[0](); pend[0] = None
                mm(PSB[SP][0:NS, 0:NS], [(KTS[:, h, :], QT(h)[:, NP:NT])], [bKTS, bREG[h]], [bPS[SP]])
                act(pex[0:NS, 0:NS], PSB[SP][0:NS, 0:NS], AF.Exp, [bPS[SP], bncn], [bpex], scale=SCALE, bias=ncn[0:NS, h:h + 1])
                tt("dve", ptn[0:NS, :], pex[0:NS, 0:NS], CAUS[0:NS, :], ALU.mult, [bpex, bC], [bptn])

                def nfn(e, h=h, ptn=ptn):
                    e.matmul(PSB[OTS][:, h * 8:(h + 1) * 8], lhsT=VTS[0:NS, h * 128:(h + 1) * 128], rhs=ptn[0:NS, :],
                             start=False, stop=True)
                    return e.matmul(PSB[DENS][:, h * 8:(h + 1) * 8], lhsT=ONEB[0:NS, :], rhs=ptn[0:NS, :],
                                    start=False, stop=True)
                P.op("pe", nfn, reads=[bVTS, bptn, bC], writes=[bPS[OTS], bPS[DENS]])
            rd, brd = sf(64); o32, bo32 = sf(64); osq, bosq = sbb(64); rs_, brs = sf(64)
            P.op("dve", lambda e, rd=rd: e.reciprocal(out=rd, in_=PSB[DENS][:, 0:64]), reads=[bPS[DENS]], writes=[brd])
            tt("dve", o32, PSB[OTS][:, 0:64], rd, ALU.mult, [bPS[OTS], brd], [bo32])
            act(osq, o32, AF.Square, [bo32], [bosq])
            mm(PSB[RP][:, 0:64], [(ONEB[:], osq)], [bosq, bC], [bPS[RP]])
            act(rs_, PSB[RP][:, 0:64], AF.Ln, [bPS[RP]], [brs], scale=1.0 / DH, bias=EPS)
            act(rs_, rs_, AF.Exp, [brs], [brs], scale=-0.5)
            tt("dve", o32, o32, rs_, ALU.mult, [bo32, brs], [bo32])
            tt("dve", XN[:, 0:8, NP:NT], o32.rearrange("p (h t) -> p h t", t=NS),
               PAR[:, po + PL_GA:po + PL_GA + 8].unsqueeze(2).to_broadcast([128, 8, NS]), ALU.mult,
               [bo32, bPAR], bXN[0:8])
            barrier()

            hal, bhal = sbb(4 * 8 * HALO)
            halv = hal.rearrange("p (r c t) -> p r c t", r=4, c=8)
            for r in range(4):
                src = rcvM[r * 64:r * 64 + HALO, :].rearrange("r c -> (r c)").rearrange("(c p t) -> p c t", p=128, t=HALO)
                dma("sp", halv[:, r], src, [bRCV], [bhal])
            for c in range(8):
                ts("dve", U(c)[:, 0:HALO], halv[:, 0, c, :], PAR[:, P_PREV:P_PREV + 1], None, ALU.mult, None, [bhal, bPAR], [bREG[8 + c]])
                for r in range(1, 4):
                    stt("dve", U(c)[:, 0:HALO], halv[:, r, c, :], PAR[:, P_PREV + r:P_PREV + r + 1], U(c)[:, 0:HALO],
                        ALU.mult, ALU.add, [bhal, bPAR], [bREG[8 + c]])
            dg = [sbb(128) for _ in range(4)]
            di = [0]
            for c in range(8):
                for j_ in range(CK):
                    td, bd = dg[di[0] % 4]; di[0] += 1
                    ts("pool", td, IDB[:], PAR[:, po + PL_CW + c * CK + j_:po + PL_CW + c * CK + j_ + 1], None, ALU.mult, None,
                       [bC, bPAR], [bd])

                    def cfn(e, td=td, c=c, j_=j_, U=U):
                        e.matmul(PSB[0][:, :], lhsT=td, rhs=U(c)[:, j_:j_ + 512], start=(j_ == 0), stop=(j_ == CK - 1))
                        e.matmul(PSB[1][:, :], lhsT=td, rhs=U(c)[:, 512 + j_:512 + j_ + 512], start=(j_ == 0), stop=(j_ == CK - 1))
                        return e.matmul(PSB[2][:, 0:NS], lhsT=td, rhs=USB[:, c, j_:j_ + NS], start=(j_ == 0), stop=(j_ == CK - 1))
                    P.op("pe", cfn, reads=[bd, bREG[8 + c], bUS], writes=[bPS[0], bPS[1], bPS[2]])
                cb = PAR[:, po + PL_CB + c:po + PL_CB + c + 1]
                act(XN[:, 8 + c, 0:512], PSB[0][:, :], AF.Identity, [bPS[0], bPAR], [bXN[8 + c]], bias=cb)
                act(XN[:, 8 + c, 512:1024], PSB[1][:, :], AF.Identity, [bPS[1], bPAR], [bXN[8 + c]], bias=cb)
                act(XN[:, 8 + c, NP:NT], PSB[2][:, 0:NS], AF.Identity, [bPS[2], bPAR], [bXN[8 + c]], bias=cb)
            sq2 = [sbb(NT), sbb(NT)]
            for c in range(8):
                t_, b_ = sq2[c % 2]
                act(t_, XN[:, 8 + c, :], AF.Square, [bXN[8 + c]], [b_])
                for n, (a, z) in enumerate(CH):
                    mm(PSB[n][:, 0:z - a], [(ONEB[:], XN[:, 8 + c, a:z])], [bXN[8 + c], bC], [bPS[n]], start=(c == 0), stop=(c == 7))
                    mm(PSB[3 + n][:, 0:z - a], [(ONEB[:], t_[:, a:z])], [b_, bC], [bPS[3 + n]], start=(c == 0), stop=(c == 7))
            mean, bmean = sf(NT); rstd2, brstd2 = sf(NT); tmp, btmp = sf(NT); sg2, bsg2 = sf(NT)
            for n, (a, z) in enumerate(CH):
                ts("dve", mean[:, a:z], PSB[n][:, 0:z - a], 1.0 / CW, None, ALU.mult, None, [bPS[n]], [bmean])
                tt("dve", tmp[:, a:z], mean[:, a:z], mean[:, a:z], ALU.mult, [bmean], [btmp])
                stt("dve", rstd2[:, a:z], PSB[3 + n][:, 0:z - a], 1.0 / CW, tmp[:, a:z], ALU.mult, ALU.subtract,
                    [bPS[3 + n], btmp], [brstd2])
            act(rstd2, rstd2, AF.Ln, [brstd2], [brstd2], bias=EPS)
            act(rstd2, rstd2, AF.Exp, [brstd2], [brstd2], scale=-0.5)
            for c in range(8):
                tt("dve", tmp, XN[:, 8 + c, :], mean, ALU.subtract, [bXN[8 + c], bmean], [btmp])
                tt("dve", tmp, tmp, rstd2, ALU.mult, [btmp, brstd2], [btmp])
                ts("dve", tmp, tmp, PAR[:, po + PL_LG + c:po + PL_LG + c + 1], PAR[:, po + PL_LB + c:po + PL_LB + c + 1],
                   ALU.mult, ALU.add, [btmp, bPAR], [btmp])
                act(sg2, tmp, AF.Sigmoid, [btmp], [bsg2])
                tt("dve", XN[:, 8 + c, :], tmp, sg2, ALU.mult, [btmp, bsg2], [bXN[8 + c]])
            barrier()

            if o_dbg is not None and l == 0 and "DBGX" not in DBG and "DBGX1" not in DBG:
                dg32, bdg32 = sf(16 * NS)
                cp("dve", dg32.rearrange("p (k t) -> p k t", t=NS), XN[:, :, NP:NT], bXN, [bdg32])
                dma("sp", o_dbg, dg32, [bdg32], [])
            lfa, blfa = sf(256); tot, btot = sf(256); cg, bcg = sf(256); car, bcar = sf(256)
            cref, bcref = sf(16); bias, bbias = sf(512); cgo, bcgo = sf(64); biaso, bbiaso = sf(128); wt, bwt = sf(256)
            lfv = lfa.rearrange("p (k h) -> p k h", h=8)
            for r in range(4):
                dma("sp", lfv[:, 8 * r:8 * r + 8, :], rcvf[r].rearrange("(k p) h -> p k h", p=128), [bRCV], [blfa])
            for blk in range(32):
                mm(PSB[4][:, blk * 8:(blk + 1) * 8], [(ONE32[:], lfv[:, blk, :])], [blfa, bC], [bPS[4]])
                mm(PSB[5][:, blk * 8:(blk + 1) * 8], [(CF[:, C_TRI:C_TRI + 128], lfv[:, blk, :])], [blfa, bC], [bPS[5]])
            cp("dve", tot, PSB[4][:, 0:256], [bPS[4]], [btot])
            P.op("dve", lambda e, car=car: e.memset(car[:, 0:8], 0.0), writes=[bcar])
            for blk in range(1, 32):
                tt("dve", car[:, blk * 8:(blk + 1) * 8], car[:, (blk - 1) * 8:blk * 8], tot[:, (blk - 1) * 8:blk * 8], ALU.add,
                   [bcar, btot], [bcar])
            tt("dve", cg, PSB[5][:, 0:256], car, ALU.add, [bPS[5], bcar], [bcg])
            for qc in range(2):
                tt("dve", wt.rearrange("p (h k) -> p h k", h=8), tot.rearrange("p (k h) -> p h k", h=8),
                   PAR[:, P_WQ + 32 * qc:P_WQ + 32 * (qc + 1)].unsqueeze(1).to_broadcast([128, 8, 32]), ALU.mult,
                   [btot, bPAR], [bwt])
                P.op("dve", lambda e, qc=qc, cref=cref, wt=wt: e.tensor_reduce(out=cref[:, qc * 8:(qc + 1) * 8], in_=wt.rearrange("p (h k) -> p h k", h=8),
                     axis=AX.X, op=ALU.add), reads=[bwt], writes=[bcref])
            biasv = bias.rearrange("p (q k h) -> p q k h", q=2, h=8)
            cgv = cg.rearrange("p (k h) -> p k h", h=8)
            for qc in range(2):
                tt("dve", biasv[:, qc], cref[:, qc * 8:(qc + 1) * 8].unsqueeze(1).to_broadcast([128, 32, 8]), cgv, ALU.subtract,
                   [bcref, bcg], [bbias])
                tt("dve", biasv[:, qc], biasv[:, qc], PAR[:, P_MB:P_MB + 32].unsqueeze(2).to_broadcast([128, 32, 8]), ALU.add,
                   [bbias, bPAR], [bbias])
            cgov = cgo.rearrange("p (k h) -> p k h", h=8)
            ts("dve", cgov, cgv[:, 0:8, :], PAR[:, P_OWN:P_OWN + 1], None, ALU.mult, None, [bcg, bPAR], [bcgo])
            for r in range(1, 4):
                stt("dve", cgov, cgv[:, 8 * r:8 * r + 8, :], PAR[:, P_OWN + r:P_OWN + r + 1], cgov, ALU.mult, ALU.add,
                    [bcg, bPAR], [bcgo])
            biasov = biaso.rearrange("p (q k h) -> p q k h", q=2, h=8)
            for qc in range(2):
                tt("dve", biasov[:, qc], cref[:, qc * 8:(qc + 1) * 8].unsqueeze(1).to_broadcast([128, 8, 8]), cgov, ALU.subtract,
                   [bcref, bcgo], [bbiaso])
            kvs = [(sbb(1024), sbb(1024)) for _ in range(3)]
            ptl = [sbb(512) for _ in range(2)]
            rd2, brd2 = sf(512); o2, bo2 = sf(512); rs2, brs2 = sf(512); osq2, bosq2 = sbb(512)
            OT = [0, 2]; DEN = [1, 3]; SB_ = [4, 5]; NSB = 6
            ki = [0]; pi = [0]; si = [0]
            for h in range(H):
                started = [False, False]
                for srcr in (0, 1, 2, 4):
                    (tk, btk), (tv, btv) = kvs[ki[0] % 3]; ki[0] += 1
                    if srcr < 4:
                        ksrc = rcvK[h // 4][srcr * 512 + (h % 4) * 128:srcr * 512 + (h % 4 + 1) * 128, :]
                        vsrc = [rcvV[i][srcr * 512:(srcr + 1) * 512, h * 128:(h + 1) * 128] for i in range(2)]
                        rb = [bRCV]
                    else:
                        ksrc = sndK[h // 4][(h % 4) * 128:(h % 4 + 1) * 128, :]
                        vsrc = [sndV[i][:, h * 128:(h + 1) * 128] for i in range(2)]
                        rb = [bSND]
                    dma("sp", tk, ksrc, rb, [btk])
                    tv3 = tv.rearrange("p (k d) -> p k d", d=128)
                    for i in range(2):
                        dma("sp", tv3[:, 4 * i:4 * i + 4, :], vsrc[i].rearrange("(k p) d -> p k d", p=128), rb, [btv])
                    for kb in range(8):
                        for qc in range(2):
                            if srcr == 4 and qc == 0 and kb >= 4:
                                continue
                            sbk = SB_[si[0] % 2]; si[0] += 1
                            mm(PSB[sbk][:, :], [(tk[:, kb * 128:(kb + 1) * 128], QT(h)[:, qc * 512:(qc + 1) * 512])],
                               [btk, bREG[h]], [bPS[sbk]])
                            tp, btp = ptl[pi[0] % 2]; pi[0] += 1
                            if srcr < 4:
                                bap = biasv[:, qc, srcr * 8 + kb, h:h + 1]; bb = bbias
                            else:
                                bap = biasov[:, qc, kb, h:h + 1]; bb = bbiaso
                            act(tp, PSB[sbk][:, :], AF.Exp, [bPS[sbk], bb], [btp], scale=SCALE, bias=bap)
                            if srcr == 4 and (kb // 4 == qc):
                                tt("pool", tp, tp, DMB[:, kb % 4, :], ALU.mult, [btp, bC], [btp])
                            last = (srcr == 4 and ((qc == 0 and kb == 3) or (qc == 1 and kb == 7)))
                            first = not started[qc]; started[qc] = True

                            def pv(e, tv=tv, tp=tp, kb=kb, qc=qc, first=first, last=last):
                                e.matmul(PSB[OT[qc]][:, :], lhsT=tv[:, kb * 128:(kb + 1) * 128], rhs=tp, start=first, stop=last)
                                return e.matmul(PSB[DEN[qc]][:, :], lhsT=ONEB[:], rhs=tp, start=first, stop=last)
                            P.op("pe", pv, reads=[btv, btp, bC], writes=[bPS[OT[qc]], bPS[DEN[qc]]])
                for qc in range(2):
                    P.op("dve", lambda e, qc=qc, rd2=rd2: e.reciprocal(out=rd2, in_=PSB[DEN[qc]][:, :]), reads=[bPS[DEN[qc]]], writes=[brd2])
                    tt("dve", o2, PSB[OT[qc]][:, :], rd2, ALU.mult, [bPS[OT[qc]], brd2], [bo2])
                    act(osq2, o2, AF.Square, [bo2], [bosq2])
                    mm(PSB[NSB][:, :], [(ONEB[:], osq2)], [bosq2, bC], [bPS[NSB]])
                    act(rs2, PSB[NSB][:, :], AF.Ln, [bPS[NSB]], [brs2], scale=1.0 / DH, bias=EPS)
                    act(rs2, rs2, AF.Exp, [brs2], [brs2], scale=-0.5)
                    stt("dve", XN[:, h, qc * 512:(qc + 1) * 512], o2, PAR[:, po + PL_GA + h:po + PL_GA + h + 1], rs2,
                        ALU.mult, ALU.mult, [bo2, brs2, bPAR], [bXN[h]])
            barrier()

            def ev_res(ti, n, a, z, ps, bps):
                tt("dve", X[:, ti, a:z], ps, X[:, ti, a:z], ALU.add, [bps, bX[ti]], [bX[ti]])
            if o_dbg is not None and l == 0 and "DBGX1" in DBG:
                dg32, bdg32 = sf(16 * NS)
                cp("dve", dg32.rearrange("p (k t) -> p k t", t=NS), XN[:, :, NP:NT], bXN, [bdg32])
                dma("sp", o_dbg, dg32, [bdg32], [])
            proj_fm(w_out, l * D, [128 * i for i in range(16)], KT, rhs_xn, bXN, ev_res)
            if o_dbg is not None and l == 0 and "DBGX1" in DBG:
                dg33, bdg33 = sf(16 * NS)
                cp("dve", dg33.rearrange("p (k t) -> p k t", t=NS), X[:, :, NP:NT], bX, [bdg33])
                dma("sp", o_dbg2, dg33, [bdg33], [])

            rmsnorm(po + PL_GMLP, xn_out)
            Hh = lambda t: REG[:, t, 0:NT]
            r32 = [sf(344), sf(344)]
            ri = [0]
            for g in range(4):
                def ev_up(ti, n, a, z, ps, bps):
                    t_, b_ = r32[ri[0] % 2]; ri[0] += 1
                    act(t_[:, 0:z - a], ps, AF.Relu, [bps], [b_])
                    tt("dve", Hh(ti)[:, a:z], t_[:, 0:z - a], t_[:, 0:z - a], ALU.mult, [b_], [bREG[ti]])
                proj_fm(w_up, l * D, [g * 2048 + 128 * i for i in range(16)], KT, rhs_xn, bXN, ev_up)
                proj_fm(w_down, l * DFF + g * 2048, [128 * i for i in range(16)], 16,
                        lambda kt, a, z: Hh(kt)[:, a:z], bREG, ev_res)
            barrier()

            rmsnorm(po + PL_GPLE, xn_out)
            bWP = bf("WP")
            gs = [sf(344), sf(344)]
            gi2 = [0]
            gate = {}

            def ev_gate(ti, n, a, z, ps, bps):
                t_, b_ = gs[gi2[0] % 2]; gi2[0] += 1
                act(t_[:, 0:z - a], ps, AF.Sigmoid, [bps], [b_])
                if ti % 2 == 0 and n == 0:
                    dma("pool", WP[:], w_ple[l * PLE:(l + 1) * PLE, ti * 128:ti * 128 + 256].rearrange("(k p) c -> p k c", p=128),
                        [], [bWP])
                k = nextps()
                mm(PSB[k][:, 0:z - a], [(WP[:, kk, (ti % 2) * 128:(ti % 2) * 128 + 128], PTE[:, kk, a:z]) for kk in range(2)],
                   [bWP, bPTE], [bPS[k]])
                tt("dve", t_[:, 0:z - a], PSB[k][:, 0:z - a], t_[:, 0:z - a], ALU.mult, [bPS[k], b_], [b_])
                tt("dve", X[:, ti, a:z], X[:, ti, a:z], t_[:, 0:z - a], ALU.add, [bX[ti], b_], [bX[ti]])
            proj_fm(w_gate, l * D, [128 * i for i in range(16)], KT, rhs_xn, bXN, ev_gate)
            if o_dbg is not None and l == 0 and "DBGX" in DBG:
                dg32, bdg32 = sf(16 * NS)
                cp("dve", dg32.rearrange("p (k t) -> p k t", t=NS), X[:, :, NP:NT], bX, [bdg32])
                dma("sp", o_dbg, dg32, [bdg32], [])
            barrier()

        yt = [sf(NT), sf(NT)]

        def y_out(kt, rstd, b_r, g):
            t_, b_ = yt[kt % 2]
            stt("dve", t_, X[:, kt, :], g, rstd, ALU.mult, ALU.mult, [bX[kt], b_r, bPAR], [b_])
            dma("sp", o_y[kt * 128:(kt + 1) * 128, :], t_, [b_], [])
        rmsnorm(P_GFIN, y_out)
    try:
        body()
    except _Stop:
        pass
    P.emit()
    es.close()
    return nc


def _finish(P, es, nc):
    P.emit()
    es.close()
    return nc


_NC = None


def kernel(**inp):
    global _NC
    inp = {k: np.asarray(v) for k, v in inp.items()}
    if _NC is None:
        _NC = build_program()
    nc = _NC
    f32 = np.float32
    cst = make_consts()
    shared = {
        "cst": cst,
        "w_in": np.ascontiguousarray(inp["w_in"].reshape(L * D, INW)),
        "w_out": np.ascontiguousarray(inp["w_out"].reshape(L * D, D)),
        "w_up": np.ascontiguousarray(inp["w_up"].reshape(L * D, DFF)),
        "w_down": np.ascontiguousarray(inp["w_down"].reshape(L * DFF, D)),
        "w_gate": np.ascontiguousarray(inp["w_ple_gate"].reshape(L * D, D)),
        "w_ple": np.ascontiguousarray(inp["w_ple"].reshape(L * PLE, D)),
    }
    if "TINYW" in DBG:
        for k in ("w_out", "w_up", "w_down", "w_gate", "w_ple"):
            shared[k] = np.zeros((128, 128), f32)
    for l in range(L):
        for hh in range(2):
            shared["ck%d_%d" % (l, hh)] = np.ascontiguousarray(
                inp["cache_k"][l][:, :, 4 * hh:4 * hh + 4].transpose(0, 2, 1, 3)).reshape(NPHYS * 64, 1024)
            shared["cv%d_%d" % (l, hh)] = np.ascontiguousarray(
                inp["cache_v"][l][:, :, 4 * hh:4 * hh + 4].transpose(0, 2, 1, 3)).reshape(NPHYS * 64, 1024)
        shared["clf%d" % l] = np.ascontiguousarray(inp["cache_logf"][l].transpose(0, 2, 1)).reshape(NPHYS * 8, 128)
    in_maps = []
    for c in range(8):
        b, j = c // 4, c % 4
        rows = slice(1024 * j, 1024 * (j + 1))
        m = dict(shared)
        m["xT"] = np.ascontiguousarray(np.concatenate([inp["x_prompt"][b, rows], inp["x_sample"][c]], 0).T.astype(f32))
        m["pT"] = np.ascontiguousarray(np.concatenate(
            [np.concatenate([inp["p_prompt"][l, b, rows], inp["p_sample"][l, c]], 0).T for l in range(L)], 0).astype(f32))
        m["scv"] = np.ascontiguousarray(np.concatenate([inp["state_conv"][l, c].T for l in range(L)], 0).astype(f32))
        m["par"] = pack_params(inp, j)
        m["ptab"] = np.ascontiguousarray(inp["page_table"][c].reshape(128, 1).astype(np.int32))
        in_maps.append(m)
    res = run_bass_kernel_spmd(nc, in_maps, core_ids=list(range(8))).results
    global _LAST
    _LAST = res

    y_prompt = np.empty((2, 4096, D), f32); y_sample = np.empty((8, 8, D), f32)
    k_prompt = np.empty((L, 2, 4096, H, DH), f32); v_prompt = np.empty((L, 2, 4096, H, DH), f32)
    lf_prompt = np.empty((L, 2, 4096, H), f32); cv_prompt = np.empty((L, 2, HALO, CW), f32)
    k_sample = np.empty((L, 8, 8, H, DH), f32); v_sample = np.empty((L, 8, 8, H, DH), f32)
    lf_sample = np.empty((L, 8, 8, H), f32); cv_sample = np.empty((L, 8, HALO, CW), f32)
    for c in range(8):
        b, j = c // 4, c % 4
        rows = slice(1024 * j, 1024 * (j + 1))
        r = res[c]
        yT = r["o_y"]
        y_prompt[b, rows] = yT[:, :NP].T
        y_sample[c] = yT[:, NP:].T
        ok = r["o_k"].reshape(L, 1024, NT); ov = r["o_v"].reshape(L, NT, 1024)
        olf = r["o_lf"].reshape(L, NT, 8); ocv = r["o_cv"].reshape(L, CW, 2 * HALO)
        for l in range(L):
            k_prompt[l, b, rows] = ok[l][:, :NP].T.reshape(1024, H, DH)
            k_sample[l, c] = ok[l][:, NP:].T.reshape(8, H, DH)
            v_prompt[l, b, rows] = ov[l][:NP].reshape(1024, H, DH)
            v_sample[l, c] = ov[l][NP:].reshape(8, H, DH)
            lf_prompt[l, b, rows] = olf[l][:NP]
            lf_sample[l, c] = olf[l][NP:]
            if j == 3:
                cv_prompt[l, b] = ocv[l][:, :HALO].T
            cv_sample[l, c] = ocv[l][:, HALO:].T
    return (y_prompt, y_sample, k_prompt, v_prompt, lf_prompt, cv_prompt, k_sample, v_sample, lf_sample, cv_sample)
```
